# Optimizing a Trainium2 kernel written in Bass

```python
import math
import jax, jax.numpy as jnp
from jax import lax
import numpy as np

D_MODEL = 2048
BATCH = 8
SEQ = 2048
DEPTH = 2

N_A_LAYERS = DEPTH // 2
N_B_LAYERS = DEPTH - N_A_LAYERS
HEAD_DIM_A = 128
N_HEADS_A = D_MODEL // HEAD_DIM_A
DILATION_PATTERNS = ((128, 1), (512, 4), (2048, 16))
N_GROUPS_A = len(DILATION_PATTERNS)
DIFF_HEAD_DIM = 128
N_HEADS_B = D_MODEL // (2 * DIFF_HEAD_DIM)
DIFF_QK_WIDTH = N_HEADS_B * 2 * DIFF_HEAD_DIM
DIFF_V_WIDTH = N_HEADS_B * 2 * DIFF_HEAD_DIM
D_FF = ((8 * D_MODEL // 3 + 127) // 128) * 128
MACARON_WEIGHT = 0.5
Q_BLOCK = 128
RMS_EPS = 1e-6
SUBLN_EPS = 1e-5

kernel_name = 'yoco_dilated_diff_macaron'


def rms_norm(x, gain, eps=RMS_EPS):
    xf = x.astype(jnp.float32)
    y = xf * lax.rsqrt(jnp.mean(xf * xf, axis=-1, keepdims=True) + eps)
    return (y * gain.astype(jnp.float32)).astype(x.dtype)


def swiglu(h, w_in, w_out):
    gate, up = jnp.split(h @ w_in, 2, axis=-1)
    return (jax.nn.silu(gate) * up) @ w_out


def alibi_slopes(n_heads):
    return 2.0 ** (-8.0 * jnp.arange(1, n_heads + 1, dtype=jnp.float32) / n_heads)


def diff_lambda_init(layer_idx):
    return 0.8 - 0.6 * math.exp(-0.3 * layer_idx)


def dilated_window_branch(q, k, v, slopes, window, dilation):
    b, s, h, dh = q.shape
    n_back = window // dilation
    sub_len = s // dilation
    n_blk = -(-sub_len // n_back)
    pad = n_blk * n_back - sub_len

    def strided(t):
        t = t.reshape(b, sub_len, dilation, h, dh).transpose(0, 2, 3, 1, 4)
        t = jnp.pad(t, ((0, 0), (0, 0), (0, 0), (0, pad), (0, 0)))
        return t.reshape(b, dilation, h, n_blk, n_back, dh)

    def with_prev_block(t):
        prev = jnp.pad(t, ((0, 0), (0, 0), (0, 0), (1, 0), (0, 0), (0, 0)))[:, :, :, :-1]
        return jnp.concatenate([prev, t], axis=4)

    qb = strided(q)
    kb = with_prev_block(strided(k))
    vb = with_prev_block(strided(v))
    scores = jnp.einsum('bchnqe,bchnke->bchnqk', qb, kb,
                        preferred_element_type=jnp.float32) * (dh ** -0.5)
    steps = jnp.arange(n_back)[:, None] + n_back - jnp.arange(2 * n_back)[None, :]
    key_idx = (jnp.arange(n_blk)[:, None, None] - 1) * n_back + jnp.arange(2 * n_back)[None, None, :]
    valid = (steps >= 0) & (steps <= n_back) & (key_idx >= 0)
    bias = -slopes[:, None, None, None] * (dilation * steps).astype(jnp.float32)
    scores = jnp.where(valid, scores + bias, -jnp.inf)
    lse = jax.nn.logsumexp(scores, axis=-1)
    probs = jnp.exp(scores - lse[..., None])
    out = jnp.einsum('bchnqk,bchnke->bchnqe', probs.astype(v.dtype), vb)
    out = out.reshape(b, dilation, h, n_blk * n_back, dh)[:, :, :, :sub_len]
    out = out.transpose(0, 3, 1, 2, 4).reshape(b, s, h, dh)
    lse = lse.reshape(b, dilation, h, n_blk * n_back)[..., :sub_len]
    lse = lse.transpose(0, 3, 1, 2).reshape(b, s, h)
    return out, lse


def dilated_attention_mixer(hn, w_qkv, w_o, slopes):
    b, s, _ = hn.shape
    qkv = (hn @ w_qkv).reshape(b, s, 3, N_GROUPS_A, N_HEADS_A, HEAD_DIM_A)
    outs, lses = [], []
    for g, (window, dilation) in enumerate(DILATION_PATTERNS):
        o, l = dilated_window_branch(qkv[:, :, 0, g], qkv[:, :, 1, g], qkv[:, :, 2, g],
                                     slopes, window, dilation)
        outs.append(o)
        lses.append(l)
    weights = jax.nn.softmax(jnp.stack(lses, axis=0), axis=0)
    merged = jnp.einsum('gbsh,gbshe->bshe', weights, jnp.stack(outs, axis=0).astype(jnp.float32))
    return merged.reshape(b, s, N_HEADS_A * HEAD_DIM_A).astype(hn.dtype) @ w_o


def shared_kv(x, gain, w_kv):
    b, s, _ = x.shape
    kv = rms_norm(x, gain) @ w_kv
    k = kv[..., :DIFF_QK_WIDTH].reshape(b, s, N_HEADS_B, 2, DIFF_HEAD_DIM)
    v = kv[..., DIFF_QK_WIDTH:].reshape(b, s, N_HEADS_B, 2 * DIFF_HEAD_DIM)
    return k, v


def differential_attention_mixer(hn, k, v, w_q, lam, subln_gain, w_o, slopes, lambda_init):
    b, s, _ = hn.shape
    q = (hn @ w_q).reshape(b, s, N_HEADS_B, 2, DIFF_HEAD_DIM)
    lam_f = lam.astype(jnp.float32)
    lam_full = (jnp.exp(jnp.sum(lam_f[0] * lam_f[1])) - jnp.exp(jnp.sum(lam_f[2] * lam_f[3]))
                + lambda_init)
    scale = DIFF_HEAD_DIM ** -0.5
    outs = []
    for start in range(0, s, Q_BLOCK):
        end = start + Q_BLOCK
        sc = jnp.einsum('bqhmd,bkhmd->bhmqk', q[:, start:end], k[:, :end],
                        preferred_element_type=jnp.float32) * scale
        qpos = jnp.arange(start, end)[:, None]
        kpos = jnp.arange(end)[None, :]
        dist = (qpos - kpos).astype(jnp.float32)
        sc = jnp.where(qpos >= kpos, sc - slopes[:, None, None, None] * dist, -jnp.inf)
        p = jax.nn.softmax(sc, axis=-1)
        attn = p[:, :, 0] - lam_full * p[:, :, 1]
        outs.append(jnp.einsum('bhqk,bkhe->bqhe', attn.astype(v.dtype), v[:, :end]))
    o = jnp.concatenate(outs, axis=1)
    o = rms_norm(o, subln_gain, SUBLN_EPS) * (1.0 - lambda_init)
    return o.reshape(b, s, DIFF_V_WIDTH) @ w_o


def setup_inputs(seed: int = 0) -> dict:
    key = jax.random.key(seed)
    ks = jax.random.split(key, 13)
    f32 = jnp.float32

    def dense(k, shape, fan_in):
        return jax.random.normal(k, shape, f32) * fan_in ** -0.5

    def gain(k, shape):
        return 1.0 + 0.02 * jax.random.normal(k, shape, f32)

    qkv_a_width = 3 * N_GROUPS_A * N_HEADS_A * HEAD_DIM_A
    return {
        'x': jax.random.normal(ks[0], (BATCH, SEQ, D_MODEL), f32),
        'norm_gains': gain(ks[1], (DEPTH, 3, D_MODEL)),
        'ffn_w_in': dense(ks[2], (DEPTH, 2, D_MODEL, 2 * D_FF), D_MODEL),
        'ffn_w_out': dense(ks[3], (DEPTH, 2, D_FF, D_MODEL), D_FF),
        'a_w_qkv': dense(ks[4], (N_A_LAYERS, D_MODEL, qkv_a_width), D_MODEL),
        'a_w_o': dense(ks[5], (N_A_LAYERS, N_HEADS_A * HEAD_DIM_A, D_MODEL), N_HEADS_A * HEAD_DIM_A),
        'kv_norm_gain': gain(ks[6], (D_MODEL,)),
        'b_w_kv': dense(ks[7], (D_MODEL, DIFF_QK_WIDTH + DIFF_V_WIDTH), D_MODEL),
        'b_w_q': dense(ks[8], (N_B_LAYERS, D_MODEL, DIFF_QK_WIDTH), D_MODEL),
        'b_lambda': 0.1 * jax.random.normal(ks[9], (N_B_LAYERS, 4, DIFF_HEAD_DIM), f32),
        'b_subln_gain': gain(ks[10], (N_B_LAYERS, 2 * DIFF_HEAD_DIM)),
        'b_w_o': dense(ks[11], (N_B_LAYERS, DIFF_V_WIDTH, D_MODEL), DIFF_V_WIDTH),
        'final_norm_gain': gain(ks[12], (D_MODEL,)),
    }


def reference(x, norm_gains, ffn_w_in, ffn_w_out, a_w_qkv, a_w_o, kv_norm_gain, b_w_kv,
              b_w_q, b_lambda, b_subln_gain, b_w_o, final_norm_gain):
    slopes_a = alibi_slopes(N_HEADS_A)
    slopes_b = alibi_slopes(N_HEADS_B)
    k_shared, v_shared = None, None
    for layer in range(DEPTH):
        x = x + MACARON_WEIGHT * swiglu(rms_norm(x, norm_gains[layer, 0]),
                                        ffn_w_in[layer, 0], ffn_w_out[layer, 0])
        hn = rms_norm(x, norm_gains[layer, 1])
        if layer < N_A_LAYERS:
            x = x + dilated_attention_mixer(hn, a_w_qkv[layer], a_w_o[layer], slopes_a)
        else:
            j = layer - N_A_LAYERS
            x = x + differential_attention_mixer(hn, k_shared, v_shared, b_w_q[j], b_lambda[j],
                                                 b_subln_gain[j], b_w_o[j], slopes_b,
                                                 diff_lambda_init(layer))
        x = x + MACARON_WEIGHT * swiglu(rms_norm(x, norm_gains[layer, 2]),
                                        ffn_w_in[layer, 1], ffn_w_out[layer, 1])
        if layer == N_A_LAYERS - 1:
            k_shared, v_shared = shared_kv(x, kv_norm_gain, b_w_kv)
    return rms_norm(x, final_norm_gain)
```

```python
import numpy as np
import concourse.bass as bass
import concourse.mybir as mybir
from concourse.bass_utils import run_bass_kernel_spmd

F32 = mybir.dt.float32
BF16 = mybir.dt.bfloat16
AF = mybir.ActivationFunctionType
ALU = mybir.AluOpType

D = 2048
S = 2048
DC = 16
F = 5504
FC = 43
NT = 4
TT = 512
NPART = 4
RMS_EPS = 1e-6
SUBLN_EPS = 1e-5
N_CORES = 8

ENGS = ("pe", "act", "dve", "pool", "sp")


class Prog:
    def __init__(self, nc):
        self.nc = nc
        self.q = {e: [] for e in ENGS}
        self.sem = {}
        self.cnt = {}
        self.waited = {e: {} for e in ENGS}
        self.nsem = 0
        self.sem_by_name = {}
        for e in ("pe", "act", "dve", "pool"):
            self.sem[e] = self.new_sem("prog_" + e)
        self.sb_off = (nc.sbuf_base + 63) // 64 * 64
        self.sb_top = nc.sbuf_top
        self.nalloc = 0

    def new_sem(self, name):
        if name in self.sem_by_name:
            return self.sem_by_name[name]
        h = self.nc.alloc_semaphore(name=name)
        self.sem_by_name[name] = h
        self.cnt[id(h)] = 0
        self.nsem += 1
        return h

    def alloc(self, shape, dtype, name=None):
        nbytes = int(np.prod(shape[1:])) * (4 if dtype == F32 else 2)
        nbytes = (nbytes + 63) // 64 * 64
        off = self.sb_off
        assert off + nbytes <= self.sb_top, ("SBUF overflow", name, off, nbytes, self.sb_top)
        self.sb_off += nbytes
        self.nalloc += 1
        t = self.nc.alloc_sbuf_tensor_at(name or f"t{self.nalloc}", list(shape), dtype, offset=off)
        return t

    def mark(self):
        return self.sb_off

    def reset(self, m):
        self.sb_off = m

    def wait(self, eng, ev):
        if ev is None:
            return
        if isinstance(ev, list):
            for x in ev:
                self.wait(eng, x)
            return
        sem, val = ev
        k = id(sem)
        if self.waited[eng].get(k, 0) >= val:
            return
        self.waited[eng][k] = val
        self.q[eng].append(lambda e: e.wait_ge(sem, val))

    def op(self, eng, fn, waits=None, signal=True):
        self.wait(eng, waits)
        if signal:
            sem = self.sem[eng]
            self.cnt[id(sem)] += 1
            val = self.cnt[id(sem)]
            self.q[eng].append(lambda e: fn(e).then_inc(sem, 1))
            return (sem, val)
        self.q[eng].append(fn)
        return None

    def dma(self, eng, out, in_, sem, waits=None, slow=False):
        self.wait(eng, waits)
        self.cnt[id(sem)] += 16
        val = self.cnt[id(sem)]
        if slow:
            self.q[eng].append(lambda e: e.dma_start(out=out, in_=in_, allow_slow_non_contiguous=True).then_inc(sem, 16))
        else:
            self.q[eng].append(lambda e: e.dma_start(out=out, in_=in_).then_inc(sem, 16))
        return (sem, val)

    def last(self, eng):
        sem = self.sem[eng]
        return (sem, self.cnt[id(sem)])

    def barrier(self, extra=None):
        evs = [self.last(e) for e in ("pe", "act", "dve", "pool")]
        if extra:
            evs = evs + [x for x in extra if x is not None]
        best = {}
        for sem, val in evs:
            if id(sem) not in best or best[id(sem)][1] < val:
                best[id(sem)] = (sem, val)
        evs = list(best.values())
        for e in ENGS:
            self.wait(e, evs)

    def finalize(self):
        nc = self.nc
        q = self.q
        with nc.Block() as block:
            @block.tensor
            def _(e):
                for f in q["pe"]:
                    f(e)

            @block.scalar
            def _(e):
                for f in q["act"]:
                    f(e)

            @block.vector
            def _(e):
                for f in q["dve"]:
                    f(e)

            @block.gpsimd
            def _(e):
                for f in q["pool"]:
                    f(e)

            @block.sync
            def _(e):
                for f in q["sp"]:
                    f(e)


class Ring:
    def __init__(self, P, n, shape, dtype, name, dma=False):
        self.n = n
        self.tiles = [P.alloc(shape, dtype, f"{name}{i}") for i in range(n)]
        self.free = [None] * n
        self.sems = [P.new_sem(f"{name}_s{i}") for i in range(n)] if dma else None
        self.i = 0

    def get(self):
        idx = self.i % self.n
        self.i += 1
        return idx


def mm_group(P, out, pairs, waits=None):
    n = len(pairs)
    ev = None
    for i, (l, r) in enumerate(pairs):
        st, sp_ = (i == 0), (i == n - 1)
        fn = (lambda e, l=l, r=r, st=st, sp_=sp_: e.matmul(out, l, r, start=st, stop=sp_))
        ev = P.op("pe", fn, waits=waits if i == 0 else None, signal=sp_)
    return ev


class Ctx:
    pass


def build_program(upto=99, single=False):
    nc = bass.Bass("TRN2", target_bir_lowering=False)
    C = Ctx()
    C.nc = nc
    dt = nc.dram_tensor
    C.x = dt("x", [S, D], F32, kind="ExternalInput").ap()
    C.norm_gains = dt("norm_gains", [2, 3, D], F32, kind="ExternalInput").ap()
    C.ffn_w_in = dt("ffn_w_in", [2, 2, D, 2 * F], F32, kind="ExternalInput").ap()
    C.ffn_w_out = dt("ffn_w_out", [2, 2, F, D], F32, kind="ExternalInput").ap()
    C.a_w_qkv = dt("a_w_qkv", [1, D, 9 * D], F32, kind="ExternalInput").ap()
    C.a_w_o = dt("a_w_o", [1, D, D], F32, kind="ExternalInput").ap()
    C.kv_norm_gain = dt("kv_norm_gain", [D], F32, kind="ExternalInput").ap()
    C.b_w_kv = dt("b_w_kv", [D, 2 * D], F32, kind="ExternalInput").ap()
    C.b_w_q = dt("b_w_q", [1, D, D], F32, kind="ExternalInput").ap()
    C.b_lambda = dt("b_lambda", [1, 4, 128], F32, kind="ExternalInput").ap()
    C.b_subln_gain = dt("b_subln_gain", [1, 256], F32, kind="ExternalInput").ap()
    C.b_w_o = dt("b_w_o", [1, D, D], F32, kind="ExternalInput").ap()
    C.final_norm_gain = dt("final_norm_gain", [D], F32, kind="ExternalInput").ap()
    C.ident_in = dt("ident", [128, 128], F32, kind="ExternalInput").ap()
    C.dm1_in = dt("dm1", [128, 256], F32, kind="ExternalInput").ap()
    C.tri_in = dt("tri", [128, 128], F32, kind="ExternalInput").ap()
    C.btab_in = dt("btab", [128, 128], F32, kind="ExternalInput").ap()
    C.out = dt("out", [S, D], F32, kind="ExternalOutput").ap()
    C.xT = dt("xT_scr", [DC, 128, S], F32).ap()
    C.mT = dt("mT_scr", [DC, 128, S], BF16).ap()
    C.kT = dt("kT_scr", [DC, 128, S], BF16).ap()
    C.V = dt("V_scr", [16, 128, D], BF16).ap()

    P = Prog(nc)
    C.P = P
    C.ps = [nc.alloc_psum_tensor(f"ps{i}", [128, 512], F32) for i in range(8)]
    C.ps_free = [None] * 8

    C.ident_f = P.alloc([128, 128], F32, "ident_f")
    C.ident_b = P.alloc([128, 128], BF16, "ident_b")
    C.ones_b = P.alloc([128, 128], BF16, "ones_b")
    C.gains = P.alloc([128, 8, DC], F32, "gains")
    C.dm1 = P.alloc([128, 2, 128], F32, "dm1")
    C.bigA = P.alloc([128, DC, S], BF16, "bigA")
    C.setup_sem = P.new_sem("setup")

    ev = P.dma("sp", C.ident_f[:], C.ident_in[:, :], C.setup_sem)
    ev = P.dma("sp", C.dm1[:], C.dm1_in.rearrange("p (a b) -> p a b", a=2), C.setup_sem)
    for w in range(6):
        l, j = divmod(w, 3)
        ev = P.dma("sp", C.gains[:, w, :], C.norm_gains[l, j].rearrange("(c p) -> p c", p=128), C.setup_sem, slow=True)
    ev = P.dma("sp", C.gains[:, 6, :], C.kv_norm_gain.rearrange("(c p) -> p c", p=128), C.setup_sem, slow=True)
    ev = P.dma("sp", C.gains[:, 7, :], C.final_norm_gain.rearrange("(c p) -> p c", p=128), C.setup_sem, slow=True)
    C.setup_ev = ev
    P.op("dve", lambda e: e.tensor_copy(C.ident_b[:], C.ident_f[:]), waits=ev)
    P.op("dve", lambda e: e.memset(C.ones_b[:], 1.0))
    P.barrier([ev])

    phases = [
        ("init", lambda: phase_init(C)),
        ("ffn00", lambda: phase_ffn(C, 0, 0)),
        ("attnA", lambda: phase_attn_a(C)),
        ("woA", lambda: phase_wo(C, C.a_w_o[0], "A")),
        ("ffn01", lambda: phase_ffn(C, 0, 1)),
        ("kv", lambda: phase_kv(C)),
        ("ffn10", lambda: phase_ffn(C, 1, 0)),
        ("attnB", lambda: phase_attn_b(C)),
        ("woB", lambda: phase_wo(C, C.b_w_o[0], "B")),
        ("ffn11", lambda: phase_ffn(C, 1, 1)),
    ]
    for i, (name, fn) in enumerate(phases):
        if i > upto:
            break
        m = P.mark()
        fn()
        P.reset(m)
    m = P.mark()
    phase_final(C)
    P.reset(m)
    P.finalize()
    return nc


def phase_init(C):
    P = C.P
    xin = Ring(P, 3, [128, D], F32, "ix", dma=True)
    xs = Ring(P, 2, [128, DC, 128], F32, "ixs", dma=True)
    stores = []
    lds = {}

    def ensure_ld(k):
        if k < S // 128 and k not in lds:
            a_ = xin.get()
            lds[k] = (a_, P.dma("sp", xin.tiles[a_][:], C.x[k * 128:(k + 1) * 128, :], xin.sems[a_], waits=xin.free[a_]))

    ensure_ld(0)
    for i in range(S // 128):
        ensure_ld(i + 1)
        a, ld = lds.pop(i)
        b = xs.get()
        evs = []
        for g in range(4):
            bank = g % 2
            ps = C.ps[bank]
            tev = None
            for k in range(4):
                c = g * 4 + k
                tev = P.op("pe", lambda e, ps=ps, k=k, c=c, a=a: e.transpose(
                    ps[:, k * 128:(k + 1) * 128], xin.tiles[a][:, c * 128:(c + 1) * 128], C.ident_f[:]),
                    waits=[ld, C.ps_free[bank]] if k == 0 else None, signal=(k == 3))
            if g % 2:
                cev = P.op("act", lambda e, ps=ps, b=b, g=g: e.copy(
                    xs.tiles[b][:, g * 4:(g + 1) * 4, :], ps[:].rearrange("p (k t) -> p k t", k=4)), waits=[tev, xs.free[b]])
            else:
                cev = P.op("dve", lambda e, ps=ps, b=b, g=g: e.tensor_copy(
                    xs.tiles[b][:, g * 4:(g + 1) * 4, :], ps[:].rearrange("p (k t) -> p k t", k=4)), waits=[tev, xs.free[b]])
            C.ps_free[bank] = cev
            evs.append(cev)
        xin.free[a] = tev
        st = P.dma("sp", C.xT[:, :, i * 128:(i + 1) * 128].rearrange("c p t -> p c t"), xs.tiles[b][:], xs.sems[b], waits=evs)
        xs.free[b] = st
        stores.append(st)
    P.barrier(stores[-2:])


def norm_gen(C, widx, dst_fn, dst_wait_fn=None):
    P = C.P
    NSUB, W = 8, 256
    xin = Ring(P, 2, [128, DC, W], F32, "nx", dma=True)
    sq = Ring(P, 3, [128, W], BF16, "nsq")
    rs = Ring(P, 2, [128, W], F32, "nrs")
    banks = [6, 7]
    lds = {}

    def load(i):
        a_ = xin.get()
        lds[i] = (a_, P.dma("sp", xin.tiles[a_][:], C.xT[:, :, i * W:(i + 1) * W].rearrange("c p t -> p c t"), xin.sems[a_],
                            waits=xin.free[a_]))

    load(0)
    for i in range(NSUB):
        if i + 1 < NSUB:
            load(i + 1)
        a, ld = lds.pop(i)
        X = xin.tiles[a]
        bank = banks[i % 2]
        ps = C.ps[bank][:, 0:W]
        mm = None
        for c in range(DC):
            k = sq.get()
            sev = P.op("act", lambda e, k=k, c=c, X=X: e.activation(sq.tiles[k][:], X[:, c, :], AF.Square),
                       waits=[ld, sq.free[k]])
            mm = P.op("pe", lambda e, k=k, c=c, ps=ps: e.matmul(ps, C.ones_b[:], sq.tiles[k][:], start=(c == 0), stop=(c == DC - 1)),
                      waits=[sev, C.ps_free[bank]] if c == 0 else [sev], signal=True)
            sq.free[k] = mm
        r = rs.get()
        R_ = rs.tiles[r]
        r1 = P.op("dve", lambda e, R_=R_, ps=ps: e.tensor_scalar(R_[:], ps, 1.0 / D, RMS_EPS, ALU.mult, ALU.add), waits=[mm, rs.free[r]])
        C.ps_free[bank] = r1
        r2 = P.op("act", lambda e, R_=R_: e.activation(R_[:], R_[:], AF.Sqrt), waits=r1)
        r3 = P.op("dve", lambda e, R_=R_: e.reciprocal(R_[:], R_[:]), waits=r2)
        dw = dst_wait_fn(i) if dst_wait_fn else None
        ev = r3
        for c in range(DC):
            ev = P.op("dve", lambda e, c=c, i=i, X=X, R_=R_: e.scalar_tensor_tensor(
                dst_fn(i, c), X[:, c, :], C.gains[:, widx, c:c + 1], R_[:], ALU.mult, ALU.mult), waits=[r3, dw] if c == 0 else None)
        xin.free[a] = ev
        rs.free[r] = ev
        yield i, ev


def norm_to_bigA(C, widx):
    evs = [ev for _, ev in norm_gen(C, widx, lambda i, c: C.bigA[:, c, i * 256:(i + 1) * 256])]
    return [evs[1], evs[3], evs[5], evs[7]]


def norm_tile(C, widx, tt, dst_fn, xin_all, sq, rstd, ld_sem, ps_bank, free_ev):
    P = C.P
    ld = P.dma("sp", xin_all[:], C.xT[:, :, tt * TT:(tt + 1) * TT].rearrange("c p t -> p c t"), ld_sem, waits=free_ev)
    ps = C.ps[ps_bank]
    mm = None
    for c in range(DC):
        k = sq.get()
        sev = P.op("act", lambda e, k=k, c=c: e.activation(sq.tiles[k][:], xin_all[:, c, :], AF.Square),
                   waits=[ld, sq.free[k]])
        mm = P.op("pe", lambda e, k=k, c=c: e.matmul(ps[:], C.ones_b[:], sq.tiles[k][:], start=(c == 0), stop=(c == DC - 1)),
                  waits=[sev, C.ps_free[ps_bank]] if c == 0 else [sev], signal=True)
        sq.free[k] = mm
    r1 = P.op("dve", lambda e: e.tensor_scalar(rstd[:], ps[:], 1.0 / D, RMS_EPS, ALU.mult, ALU.add), waits=[mm, free_ev])
    C.ps_free[ps_bank] = r1
    r2 = P.op("act", lambda e: e.activation(rstd[:], rstd[:], AF.Sqrt), waits=r1)
    r2 = P.op("dve", lambda e: e.reciprocal(rstd[:], rstd[:]), waits=r2)
    ev = r2
    for c in range(DC):
        ev = P.op("dve", lambda e, c=c: e.scalar_tensor_tensor(
            dst_fn(c), xin_all[:, c, :], C.gains[:, widx, c:c + 1], rstd[:], ALU.mult, ALU.mult), waits=r2 if c == 0 else None)
    return ev


class WStream:
    def __init__(self, P, ns=3, elems=2048, name="wst", init_wait=None):
        self.P = P
        self.st = Ring(P, ns, [128, elems], F32, name, dma=True)
        self.st.free = [init_wait] * ns

    def load(self, dst, src, a, b, dst_free):
        P = self.P
        k = self.st.get()
        view = self.st.tiles[k][:, :a * b].rearrange("p (a b) -> p a b", a=a)
        ld = P.dma("sp", view, src, self.st.sems[k], waits=self.st.free[k])
        cv = P.op("pool", lambda e: e.tensor_copy(dst, view), waits=[ld, dst_free])
        self.st.free[k] = cv
        return cv


def phase_ffn(C, l, j):
    P = C.P
    widx = l * 3 + (0 if j == 0 else 2)
    w_in = C.ffn_w_in[l, j]
    w_out = C.ffn_w_out[l, j]
    NP = 11
    aT = P.alloc([128, NP, S], BF16, "aT")
    m0 = P.mark()
    P.reset(m0 - NP * S * 2)
    hn_ev = norm_to_bigA(C, widx)
    assert P.mark() <= m0
    P.reset(m0)
    RG, RD, PF = 4, 4, 3
    ws = WStream(P, 3, 2048, "wst")
    wgu = Ring(P, RG, [128, 2, DC, 128], BF16, "wgu")
    wo = Ring(P, RD, [128, NP, 128], BF16, "wo")
    sg = Ring(P, 2, [128, TT], F32, "sg")
    xi = Ring(P, 6, [128, TT], F32, "xi", dma=True)
    xo = Ring(P, 3, [128, TT], F32, "xo", dma=True)
    psz = [11, 11, 11, 10]
    blocks = []
    c0 = 0
    for part in range(NPART):
        n = psz[part]
        for ci in range(n):
            blocks.append(("gu", part, ci, c0 + ci, n, c0))
        for jd in range(DC):
            blocks.append(("dn", part, jd, None, n, c0))
        c0 += n
    kidx = {"gu": 0, "dn": 0}
    done = {"gu": 0, "dn": 0}
    ready = {}

    def issue(m):
        kind, part, a, c, n, cb = blocks[m]
        q = kidx[kind]
        kidx[kind] += 1
        if kind == "gu":
            sl = q % RG
            assert q < RG or done["gu"] > q - RG
            e1 = ws.load(wgu.tiles[sl][:, 0, :, :], w_in[:, c * 128:(c + 1) * 128].rearrange("(k p) n -> p k n", p=128), DC, 128, wgu.free[sl])
            e2 = ws.load(wgu.tiles[sl][:, 1, :, :], w_in[:, F + c * 128:F + (c + 1) * 128].rearrange("(k p) n -> p k n", p=128), DC, 128, None)
            ready[m] = (sl, [e1, e2])
        else:
            sl = q % RD
            assert q < RD or done["dn"] > q - RD
            e1 = ws.load(wo.tiles[sl][:, :n, :], w_out[cb * 128:(cb + n) * 128, a * 128:(a + 1) * 128].rearrange("(c p) n -> p c n", p=128), n, 128, wo.free[sl])
            ready[m] = (sl, [e1])

    store_ev = {}
    PFX = 4
    lxq = {}
    lx_next = {}

    def ensure_lx(part, upto):
        k = lx_next.get(part, 0)
        while k < DC * NT and k <= upto:
            jd_, tt_ = divmod(k, NT)
            xa_ = xi.get()
            lxq[(part, k)] = (xa_, P.dma("sp", xi.tiles[xa_][:], C.xT[jd_, :, tt_ * TT:(tt_ + 1) * TT], xi.sems[xa_],
                                         waits=[xi.free[xa_], store_ev.get((jd_, tt_))]))
            k += 1
        lx_next[part] = k

    aT_free = [None] * NP * NT
    aT_ready = {}
    GB, UB, YB = [0, 1], [2, 3], [4, 5]
    it = 0
    yit = 0
    nxt = 0
    for m, (kind, part, a, c, n, cb) in enumerate(blocks):
        while nxt < len(blocks) and nxt <= m + PF:
            issue(nxt)
            nxt += 1
        sl, wev = ready.pop(m)
        if kind == "gu":
            ci = a
            w = wgu.tiles[sl]
            for tt in range(NT):
                gb = GB[it % 2]
                ub = UB[it % 2]
                it += 1
                tsl = slice(tt * TT, (tt + 1) * TT)
                gev = mm_group(P, C.ps[gb][:], [(w[:, 0, k, :], C.bigA[:, k, tsl]) for k in range(DC)],
                               waits=[wev, hn_ev[tt], C.ps_free[gb]])
                uev = mm_group(P, C.ps[ub][:], [(w[:, 1, k, :], C.bigA[:, k, tsl]) for k in range(DC)],
                               waits=[C.ps_free[ub]])
                s = sg.get()
                aev = P.op("act", lambda e, s=s, gb=gb: e.activation(sg.tiles[s][:], C.ps[gb][:], AF.Silu), waits=[gev, sg.free[s]])
                C.ps_free[gb] = aev
                dev = P.op("dve", lambda e, s=s, ub=ub, ci=ci, tsl=tsl: e.tensor_tensor(aT[:, ci, tsl], sg.tiles[s][:], C.ps[ub][:], ALU.mult),
                           waits=[aev, uev, aT_free[ci * NT + tt], hn_ev[NT - 1]])
                C.ps_free[ub] = dev
                sg.free[s] = dev
                aT_ready[(ci, tt)] = dev
            wgu.free[sl] = uev
            done["gu"] += 1
        else:
            jd = a
            w = wo.tiles[sl]
            for tt in range(NT):
                yb = YB[yit % 2]
                yit += 1
                tsl = slice(tt * TT, (tt + 1) * TT)
                ensure_lx(part, jd * NT + tt + PFX)
                xa, lx = lxq.pop((part, jd * NT + tt))
                yev = mm_group(P, C.ps[yb][:], [(w[:, ci, :], aT[:, ci, tsl]) for ci in range(n)],
                               waits=[wev, C.ps_free[yb]] + [aT_ready[(ci, tt)] for ci in range(n)])
                b = xo.get()
                dev = P.op("dve", lambda e, xa=xa, b=b, yb=yb: e.scalar_tensor_tensor(
                    xo.tiles[b][:], C.ps[yb][:], 0.5, xi.tiles[xa][:], ALU.mult, ALU.add), waits=[yev, lx, xo.free[b]])
                C.ps_free[yb] = dev
                xi.free[xa] = dev
                st = P.dma("sp", C.xT[jd, :, tsl], xo.tiles[b][:], xo.sems[b], waits=dev)
                xo.free[b] = st
                store_ev[(jd, tt)] = st
                if jd == DC - 1:
                    for ci in range(n):
                        aT_free[ci * NT + tt] = yev
            wo.free[sl] = yev
            done["dn"] += 1
    P.barrier(list(store_ev.values()))


def evac(P, eng, out, in_, waits=None):
    if eng == "act":
        return P.op("act", lambda e: e.copy(out, in_), waits=waits)
    return P.op(eng, lambda e: e.tensor_copy(out, in_), waits=waits)


def run_norm(C, widx, tagname):
    m = C.P.mark()
    hn_ev = norm_to_bigA(C, widx)
    return hn_ev, m


def phase_wo(C, w_o, tagname):
    P = C.P
    ldsem = P.new_sem("wold" + tagname)
    lds = []
    for q in range(4):
        lds.append(P.dma("sp", C.bigA[:, q * 4:(q + 1) * 4, :], C.mT[q * 4:(q + 1) * 4].rearrange("c p t -> p c t"), ldsem))
    lds = [lds[-1]]
    R, PF = 4, 3
    ws = WStream(P, 3, 2048, "wst")
    wr = Ring(P, R, [128, DC, 128], BF16, "wow")
    xi = Ring(P, 6, [128, TT], F32, "xi", dma=True)
    xo = Ring(P, 3, [128, TT], F32, "xo", dma=True)
    ready = {}
    done = [0]

    def issue(m):
        sl = m % R
        assert m < R or done[0] > m - R
        ready[m] = (sl, ws.load(wr.tiles[sl][:], w_o[:, m * 128:(m + 1) * 128].rearrange("(k p) n -> p k n", p=128), DC, 128, wr.free[sl]))

    YB = [4, 5]
    stores = []
    nxt = 0
    yit = 0
    lxq = {}
    lxn = [0]

    def ensure_lx(upto):
        k = lxn[0]
        while k < DC * NT and k <= upto:
            jd_, tt_ = divmod(k, NT)
            xa_ = xi.get()
            lxq[k] = (xa_, P.dma("sp", xi.tiles[xa_][:], C.xT[jd_, :, tt_ * TT:(tt_ + 1) * TT], xi.sems[xa_], waits=xi.free[xa_]))
            k += 1
        lxn[0] = k

    for jd in range(DC):
        while nxt < DC and nxt <= jd + PF:
            issue(nxt)
            nxt += 1
        sl, wev = ready.pop(jd)
        w = wr.tiles[sl]
        for tt in range(NT):
            yb = YB[yit % 2]
            yit += 1
            tsl = slice(tt * TT, (tt + 1) * TT)
            ensure_lx(jd * NT + tt + 4)
            xa, lx = lxq.pop(jd * NT + tt)
            yev = mm_group(P, C.ps[yb][:], [(w[:, k, :], C.bigA[:, k, tsl]) for k in range(DC)],
                           waits=[wev, C.ps_free[yb]] + lds)
            b = xo.get()
            dev = P.op("dve", lambda e, xa=xa, b=b, yb=yb: e.tensor_tensor(
                xo.tiles[b][:], C.ps[yb][:], xi.tiles[xa][:], ALU.add), waits=[yev, lx, xo.free[b]])
            C.ps_free[yb] = dev
            xi.free[xa] = dev
            st = P.dma("sp", C.xT[jd, :, tsl], xo.tiles[b][:], xo.sems[b], waits=dev)
            xo.free[b] = st
            stores.append(st)
        wr.free[sl] = yev
        done[0] += 1
    P.barrier(stores)


def phase_attn_a(C):
    P = C.P
    w_qkv = C.a_w_qkv[0]
    hn_ev, m_norm = run_norm(C, 1, "A")
    P.reset(m_norm)
    hn_all = list(hn_ev)
    R = 3
    ws = WStream(P, 3, 2048, "wst", init_wait=hn_ev[-1])
    wr = Ring(P, R, [128, 3, DC, 128], BF16, "wqkv")
    qkv = Ring(P, 2, [128, 3, S], BF16, "qkvT")
    Vb = Ring(P, 2, [128, 16, 128], BF16, "Vb")
    ET = Ring(P, 2, [128, 2, 128], F32, "ET")
    ex = Ring(P, 4, [128, 2, 128], F32, "ex")
    PT = Ring(P, 4, [128, 2, 128], BF16, "PT")
    accO = P.alloc([128, S], F32, "accO")
    accD = P.alloc([128, S], F32, "accD")
    rD = P.alloc([128, S], F32, "rD")
    mh = Ring(P, 2, [128, S], BF16, "mh", dma=True)
    dil = [1, 4, 16]
    scale = 128.0 ** -0.5
    items = [(h, g) for h in range(16) for g in range(3)]
    NI = len(items)
    ready = {}
    done = [0]
    issued = [0]
    st8 = {}
    sh = {"pit": 0, "sit": 0, "oit": 0, "acc_ev": None, "mh_ev": None, "acc_last": None}

    def issue_upto(k):
        while issued[0] < NI and issued[0] <= k:
            m = issued[0]
            h, g = items[m]
            sl = m % R
            assert m < R or done[0] > m - R
            evs = []
            for s_ in range(3):
                col = ((s_ * 3 + g) * 16 + h) * 128
                evs.append(ws.load(wr.tiles[sl][:, s_, :, :], w_qkv[:, col:col + 128].rearrange("(k p) n -> p k n", p=128), DC, 128,
                                   wr.free[sl] if s_ == 0 else None))
            ready[m] = (sl, evs)
            issued[0] += 1

    def gen_proj(m):
        h, g = items[m]
        issue_upto(m + 1)
        sl, wev = ready.pop(m)
        w = wr.tiles[sl]
        d = dil[g]
        qi = qkv.get()
        T3 = qkv.tiles[qi]
        pev = []
        gev = None
        for s_ in range(3):
            for tt in range(NT):
                pb = sh["pit"] % 2
                sh["pit"] += 1
                tsl = slice(tt * TT, (tt + 1) * TT)
                gev = mm_group(P, C.ps[pb][:], [(w[:, s_, k, :], C.bigA[:, k, tsl]) for k in range(DC)],
                               waits=[wev, C.ps_free[pb]] + hn_all)
                n_ = TT // d
                dst = T3[:, s_, :].rearrange("p (c j) -> p c j", c=d)[:, :, tt * n_:(tt + 1) * n_]
                src = C.ps[pb][:].rearrange("p (j c) -> p c j", c=d)
                eng = "act"
                cev = evac(P, eng, dst, src, waits=[gev, qkv.free[qi]])
                C.ps_free[pb] = cev
                pev.append(cev)
                if s_ * NT + tt == 11:
                    wr.free[sl] = gev
                    done[0] += 1
                    st8[m] = (qi, T3, pev)
                yield

    def gen_attn(m):
        h, g = items[m]
        qi, T3, pev = st8.pop(m)
        d = dil[g]
        bpc = 16 // d
        slope = 2.0 ** (-8.0 * (h + 1) / 16.0)
        vi = Vb.get()
        vev = []
        for q4 in range(4):
            psb = C.ps[2][:].bitcast(BF16)
            tev = None
            for k in range(4):
                b = q4 * 4 + k
                tev = P.op("pe", lambda e, psb=psb, k=k, b=b, T3=T3: e.transpose(
                    psb[:, k * 128:(k + 1) * 128], T3[:, 2, b * 128:(b + 1) * 128], C.ident_b[:]),
                    waits=(pev + [C.ps_free[2]]) if k == 0 else None, signal=(k == 3))
            cev = evac(P, "dve", Vb.tiles[vi][:, q4 * 4:(q4 + 1) * 4, :], psb[:, 0:512].rearrange("p (k t) -> p k t", k=4),
                       waits=[tev, Vb.free[vi]])
            C.ps_free[2] = cev
            vev.append(cev)
            if q4 % 2 == 1:
                yield
        ei = ET.get()
        etev = P.op("act", lambda e, ei=ei, sc=slope * d: e.activation(ET.tiles[ei][:], C.dm1[:], AF.Exp, scale=sc), waits=ET.free[ei])
        stage = {}

        def stage1(qb):
            has_prev = (qb % bpc) != 0
            nk = 2 if has_prev else 1
            sb = 3 + (sh["sit"] % 2)
            sh["sit"] += 1
            stp = C.ps[sb]
            qs = T3[:, 0, qb * 128:(qb + 1) * 128]
            sev = P.op("pe", lambda e, stp=stp, qs=qs, qb=qb, T3=T3: e.matmul(
                stp[:, 0:128], T3[:, 1, qb * 128:(qb + 1) * 128], qs, start=True, stop=True),
                waits=pev + [C.ps_free[sb]], signal=not has_prev)
            if has_prev:
                sev = P.op("pe", lambda e, stp=stp, qs=qs, qb=qb, T3=T3: e.matmul(
                    stp[:, 128:256], T3[:, 1, (qb - 1) * 128:qb * 128], qs, start=True, stop=True))
            xi_ = ex.get()
            xev = P.op("act", lambda e, xi_=xi_, stp=stp, nk=nk: e.activation(
                ex.tiles[xi_][:, 0:nk, :], stp[:, 0:nk * 128].rearrange("p (a b) -> p a b", a=nk), AF.Exp, scale=scale),
                waits=[sev, ex.free[xi_]])
            C.ps_free[sb] = xev
            pi_ = PT.get()
            mev = P.op("dve", lambda e, xi_=xi_, pi_=pi_, ei=ei, nk=nk: e.tensor_tensor(
                PT.tiles[pi_][:, 0:nk, :], ex.tiles[xi_][:, 0:nk, :], ET.tiles[ei][:, 0:nk, :], ALU.mult),
                waits=[xev, etev, PT.free[pi_]])
            ex.free[xi_] = mev
            stage[qb] = (has_prev, nk, pi_, mev)

        def stage2(qb):
            has_prev, nk, pi_, mev = stage.pop(qb)
            oit = sh["oit"]
            ob = 5 + ((oit // 2) % 2)
            half = oit % 2
            sh["oit"] += 1
            odp = C.ps[ob]
            o_sl = odp[:, half * 128:(half + 1) * 128]
            d_sl = odp[:, 256 + half * 128:256 + (half + 1) * 128]
            pv = P.op("pe", lambda e, o_sl=o_sl, vi=vi, qb=qb, pi_=pi_, nk=nk: e.matmul(
                o_sl, Vb.tiles[vi][:, qb, :], PT.tiles[pi_][:, 0, :], start=True, stop=(nk == 1)),
                waits=[mev] + vev + ([C.ps_free[ob]] if half == 0 else []), signal=False)
            if has_prev:
                pv = P.op("pe", lambda e, o_sl=o_sl, vi=vi, qb=qb, pi_=pi_: e.matmul(
                    o_sl, Vb.tiles[vi][:, qb - 1, :], PT.tiles[pi_][:, 1, :], start=False, stop=True), signal=False)
            pv = P.op("pe", lambda e, d_sl=d_sl, pi_=pi_, nk=nk: e.matmul(
                d_sl, C.ones_b[:], PT.tiles[pi_][:, 0, :], start=True, stop=(nk == 1)), signal=(nk == 1))
            if has_prev:
                pv = P.op("pe", lambda e, d_sl=d_sl, pi_=pi_: e.matmul(
                    d_sl, C.ones_b[:], PT.tiles[pi_][:, 1, :], start=False, stop=True))
            PT.free[pi_] = pv
            if half == 1:
                q0 = qb - 1
                if d == 1:
                    dO = accO[:, q0 * 128:(q0 + 2) * 128]
                    dD = accD[:, q0 * 128:(q0 + 2) * 128]
                    sO = odp[:, 0:256]
                    sD = odp[:, 256:512]
                elif d == 4:
                    c_ = q0 // 4
                    j0 = (q0 % 4) * 128
                    dO = accO[:].rearrange("p (j c) -> p c j", c=4)[:, c_, j0:j0 + 256]
                    dD = accD[:].rearrange("p (j c) -> p c j", c=4)[:, c_, j0:j0 + 256]
                    sO = odp[:, 0:256]
                    sD = odp[:, 256:512]
                else:
                    dO = accO[:].rearrange("p (j c) -> p c j", c=16)[:, q0:q0 + 2, :]
                    dD = accD[:].rearrange("p (j c) -> p c j", c=16)[:, q0:q0 + 2, :]
                    sO = odp[:, 0:256].rearrange("p (a b) -> p a b", a=2)
                    sD = odp[:, 256:512].rearrange("p (a b) -> p a b", a=2)
                if g == 0:
                    a1 = evac(P, "act", dO, sO, waits=[pv, sh["mh_ev"]])
                    a2 = evac(P, "act", dD, sD, waits=[pv, sh["mh_ev"]])
                else:
                    a1 = P.op("dve", lambda e, dO=dO, sO=sO: e.tensor_tensor(dO, dO, sO, ALU.add), waits=[pv, sh["acc_ev"]])
                    a2 = P.op("dve", lambda e, dD=dD, sD=sD: e.tensor_tensor(dD, dD, sD, ALU.add), waits=[pv, sh["acc_ev"]])
                C.ps_free[ob] = [a1, a2]
                sh["acc_last"] = [a1, a2]
            return pv, mev

        stage1(0)
        stage1(1)
        pv = mev = None
        for qb in range(16):
            if qb + 2 < 16:
                stage1(qb + 2)
            pv, mev = stage2(qb)
            yield
        qkv.free[qi] = [pv]
        Vb.free[vi] = pv
        ET.free[ei] = mev
        sh["acc_ev"] = sh["acc_last"]
        if g == 2:
            mi = mh.get()
            al = sh["acc_last"]
            r1 = P.op("dve", lambda e: e.reciprocal(rD[:], accD[:]), waits=al)
            r2 = P.op("dve", lambda e, mi=mi: e.tensor_tensor(mh.tiles[mi][:], accO[:], rD[:], ALU.mult), waits=[r1, mh.free[mi]] + al)
            sh["mh_ev"] = r2
            st = P.dma("sp", C.mT[h], mh.tiles[mi][:], mh.sems[mi], waits=r2)
            mh.free[mi] = st
        yield

    def step(gen):
        if gen is None:
            return None
        try:
            next(gen)
            return gen
        except StopIteration:
            return None

    g0 = gen_proj(0)
    while g0 is not None:
        g0 = step(g0)
    for m in range(NI):
        gB = gen_attn(m)
        gA = gen_proj(m + 1) if m + 1 < NI else None
        while gA is not None or gB is not None:
            gA = step(gA)
            gB = step(gB)
            gB = step(gB)
    P.barrier([mh.free[0], mh.free[1]])


def phase_kv(C):
    P = C.P
    w_kv = C.b_w_kv
    hn_ev, m_norm = run_norm(C, 6, "KV")
    P.reset(m_norm)
    hn_all = list(hn_ev)
    R, PF = 4, 3
    ws = WStream(P, 3, 2048, "wst", init_wait=hn_ev[-1])
    wr = Ring(P, R, [128, DC, 128], BF16, "wk")
    ko = Ring(P, 2, [128, S], BF16, "ko", dma=True)
    wv = Ring(P, 2, [128, DC, 512], BF16, "wv")
    vo = Ring(P, 3, [128, 512], BF16, "vo", dma=True)
    ready = {}
    done = [0]

    def issue(m):
        sl = m % R
        assert m < R or done[0] > m - R
        ready[m] = (sl, ws.load(wr.tiles[sl][:], w_kv[:, m * 128:(m + 1) * 128].rearrange("(k p) n -> p k n", p=128), DC, 128, wr.free[sl]))

    def issue_v(j):
        sl = j % 2
        evs = []
        for q in range(4):
            evs.append(ws.load(wv.tiles[sl][:, q * 4:(q + 1) * 4, :],
                               w_kv[q * 512:(q + 1) * 512, D + j * 512:D + (j + 1) * 512].rearrange("(k p) n -> p k n", p=128),
                               4, 512, wv.free[sl] if q == 0 else None))
        return sl, evs

    nxt = 0
    pit = 0
    stores = []
    for m in range(16):
        while nxt < 16 and nxt <= m + PF:
            issue(nxt)
            nxt += 1
        sl, wev = ready.pop(m)
        w = wr.tiles[sl]
        ki = ko.get()
        cevs = []
        for tt in range(NT):
            pb = pit % 2
            pit += 1
            tsl = slice(tt * TT, (tt + 1) * TT)
            gev = mm_group(P, C.ps[pb][:], [(w[:, k, :], C.bigA[:, k, tsl]) for k in range(DC)],
                           waits=[wev, C.ps_free[pb]] + hn_all)
            cev = evac(P, "act" if tt % 2 == 0 else "dve", ko.tiles[ki][:, tsl], C.ps[pb][:], waits=[gev, ko.free[ki]])
            C.ps_free[pb] = cev
            cevs.append(cev)
        wr.free[sl] = gev
        done[0] += 1
        st = P.dma("sp", C.kT[m], ko.tiles[ki][:], ko.sems[ki], waits=cevs)
        ko.free[ki] = st
        stores.append(st)
    vready = {0: issue_v(0)}
    for j in range(4):
        if j + 1 < 4:
            vready[j + 1] = issue_v(j + 1)
        sl, wev = vready.pop(j)
        w = wv.tiles[sl]
        for ti in range(16):
            pb = pit % 2
            pit += 1
            gev = mm_group(P, C.ps[pb][:], [(C.bigA[:, k, ti * 128:(ti + 1) * 128], w[:, k, :]) for k in range(DC)],
                           waits=wev + [C.ps_free[pb]] + hn_all)
            vi = vo.get()
            cev = evac(P, "act" if ti % 2 == 0 else "dve", vo.tiles[vi][:], C.ps[pb][:], waits=[gev, vo.free[vi]])
            C.ps_free[pb] = cev
            st = P.dma("sp", C.V[ti, :, j * 512:(j + 1) * 512], vo.tiles[vi][:], vo.sems[vi], waits=cev)
            vo.free[vi] = st
            stores.append(st)
        wv.free[sl] = gev
    P.barrier(stores)


def phase_attn_b(C):
    import math
    P = C.P
    w_q = C.b_w_q[0]
    lam_init = 0.8 - 0.6 * math.exp(-0.3 * 1)
    scale = 128.0 ** -0.5
    hn_ev, m_norm = run_norm(C, 4, "B")
    P.reset(m_norm)
    hn_all = list(hn_ev)
    R, PF = 3, 2
    ws = WStream(P, 3, 2048, "wst", init_wait=hn_ev[-1])
    wr = Ring(P, R, [128, 2, DC, 128], BF16, "wq")
    qT = Ring(P, 2, [128, 2, S], BF16, "qT")
    kT = Ring(P, 2, [128, 2, S], BF16, "kTb", dma=True)
    Vh = Ring(P, 2, [128, 16, 257], BF16, "Vh", dma=True)
    PT = Ring(P, 14, [128, 128], BF16, "PTb")
    osb = Ring(P, 2, [128, 256], F32, "osb")
    onb = Ring(P, 2, [128, 256], BF16, "onb")
    sm = Ring(P, 2, [128, 4], F32, "smb")
    junk = P.alloc([128, 256], F32, "junk")
    oT = Ring(P, 2, [128, 2, S], BF16, "oTb", dma=True)
    lam = P.alloc([128, 4, 128], F32, "lam")
    ltmp = P.alloc([128, 2, 128], F32, "ltmp")
    lsc = P.alloc([128, 4], F32, "lsc")
    gs = P.alloc([128, 256], F32, "gsub")
    tri_b = P.alloc([128, 128], BF16, "tri_b")
    tri_f = P.alloc([128, 128], F32, "tri_f")
    btab = P.alloc([128, 128], F32, "btab")
    csem = P.new_sem("bconst")
    cw = hn_ev[-1]
    P.dma("sp", lam[:], C.b_lambda[0].rearrange("a b -> (a b)").partition_broadcast(128).rearrange("p (a b) -> p a b", a=4), csem, waits=cw)
    P.dma("sp", gs[:], C.b_subln_gain[0].partition_broadcast(128), csem)
    P.dma("sp", tri_f[:], C.tri_in[:, :], csem)
    cl = P.dma("sp", btab[:], C.btab_in[:, :], csem)
    P.op("dve", lambda e: e.tensor_copy(tri_b[:], tri_f[:]), waits=cl)
    P.op("dve", lambda e: e.tensor_tensor(ltmp[:, 0, :], lam[:, 0, :], lam[:, 1, :], ALU.mult))
    l1 = P.op("dve", lambda e: e.tensor_tensor(ltmp[:, 1, :], lam[:, 2, :], lam[:, 3, :], ALU.mult))
    l2 = P.op("dve", lambda e: e.reduce_sum(lsc[:, 0:2], ltmp[:], mybir.AxisListType.X), waits=l1)
    l3 = P.op("act", lambda e: e.activation(lsc[:, 2:4], lsc[:, 0:2], AF.Exp), waits=l2)
    l4 = P.op("dve", lambda e: e.tensor_tensor(lsc[:, 0:1], lsc[:, 3:4], lsc[:, 2:3], ALU.subtract), waits=l3)
    l5 = P.op("dve", lambda e: e.tensor_scalar_add(lsc[:, 0:1], lsc[:, 0:1], -lam_init), waits=l4)
    g1 = P.op("dve", lambda e: e.tensor_scalar_mul(gs[:], gs[:], 1.0 - lam_init), waits=cl)
    const_ev = [l5, g1]
    for i in range(2):
        g1 = P.op("dve", lambda e, i=i: e.memset(Vh.tiles[i][:, :, 256:257], 1.0), waits=cw)
    ones_ev = g1
    ready = {}
    done = [0]

    issued = [0]

    def issue_upto(k):
        while issued[0] < 8 and issued[0] <= k:
            h = issued[0]
            sl = h % R
            assert h < R or done[0] > h - R
            evs = []
            for m_ in range(2):
                col = (h * 2 + m_) * 128
                evs.append(ws.load(wr.tiles[sl][:, m_, :, :], w_q[:, col:col + 128].rearrange("(k p) n -> p k n", p=128), DC, 128,
                                   wr.free[sl] if m_ == 0 else None))
            ready[h] = (sl, evs)
            issued[0] += 1

    OB = [[4, 5], [6, 7]]
    sh = {"pit": 0, "scnt": 0}
    hst = {}

    def gen_proj(h):
        issue_upto(h + 1)
        sl, wev = ready.pop(h)
        w = wr.tiles[sl]
        ki = kT.get()
        kld = P.dma("sp", kT.tiles[ki][:], C.kT[h * 2:h * 2 + 2].rearrange("m p t -> p m t"), kT.sems[ki], waits=[kT.free[ki], cw])
        vi = Vh.get()
        vld = P.dma("sp", Vh.tiles[vi][:, :, 0:256], C.V[:, :, h * 256:(h + 1) * 256].rearrange("t p e -> p t e"), Vh.sems[vi],
                    waits=[Vh.free[vi], cw])
        qi = qT.get()
        Q = qT.tiles[qi]
        pev = []
        for m_ in range(2):
            for tt in range(NT):
                pb = 0
                tsl = slice(tt * TT, (tt + 1) * TT)
                gev = mm_group(P, C.ps[pb][:], [(w[:, m_, k, :], C.bigA[:, k, tsl]) for k in range(DC)],
                               waits=wev + [C.ps_free[pb]] + hn_all)
                cev = evac(P, "dve", Q[:, m_, tsl], C.ps[pb][:], waits=[gev, qT.free[qi]])
                C.ps_free[pb] = cev
                pev.append(cev)
                if m_ == 1 and tt == NT - 1:
                    wr.free[sl] = gev
                    done[0] += 1
                    hst[h] = (ki, kld, vi, vld, qi, Q, pev)
                yield

    def gen_attn(h):
        ki, kld, vi, vld, qi, Q, pev = hst.pop(h)
        oi = oT.get()
        OT = oT.tiles[oi]
        tr_evs = []
        steps = []
        for qt in range(16):
            for m_ in range(2):
                for k0 in range(0, qt + 1, 4):
                    steps.append((qt, m_, list(range(k0, min(k0 + 4, qt + 1)))))
        pend = {}
        pvs_of = {}

        def stage1(i):
            qt, m_, kts = steps[i]
            qsl = slice(qt * 128, (qt + 1) * 128)
            sbank = 1 + (sh["scnt"] % 3)
            sh["scnt"] += 1
            sev = None
            for i_, kt in enumerate(kts):
                stp = C.ps[sbank][:, i_ * 128:(i_ + 1) * 128]
                sev = P.op("pe", lambda e, stp=stp, m_=m_, kt=kt, qsl=qsl: e.matmul(
                    stp, kT.tiles[ki][:, m_, kt * 128:(kt + 1) * 128], Q[:, m_, qsl], start=True, stop=True),
                    waits=(pev + [kld, C.ps_free[sbank]]) if i_ == 0 else None, signal=(i_ == len(kts) - 1))
            xevs = []
            pis = []
            last_x = None
            for i_, kt in enumerate(kts):
                stp = C.ps[sbank][:, i_ * 128:(i_ + 1) * 128]
                pi_ = PT.get()
                pis.append(pi_)
                bcol = h * 16 + (qt - kt)
                xev = P.op("act", lambda e, pi_=pi_, stp=stp, bcol=bcol: e.activation(
                    PT.tiles[pi_][:], stp, AF.Exp, bias=btab[:, bcol:bcol + 1], scale=scale),
                    waits=[sev, PT.free[pi_], cl])
                last_x = xev
                if kt == qt:
                    xev = P.op("dve", lambda e, pi_=pi_: e.tensor_tensor(PT.tiles[pi_][:], PT.tiles[pi_][:], tri_b[:], ALU.mult), waits=xev)
                xevs.append(xev)
            C.ps_free[sbank] = last_x
            pend[i] = (pis, xevs)

        def stage2(i):
            qt, m_, kts = steps[i]
            obs = OB[qt % 2]
            ob = obs[m_]
            pis, xevs = pend.pop(i)
            pv = None
            for i_, kt in enumerate(kts):
                pi_ = pis[i_]
                pv = P.op("pe", lambda e, ob=ob, pi_=pi_, kt=kt, qt=qt: e.matmul(
                    C.ps[ob][:, 0:257], PT.tiles[pi_][:], Vh.tiles[vi][:, kt, :], start=(kt == 0), stop=(kt == qt)),
                    waits=[xevs[i_], vld, ones_ev] + ([C.ps_free[ob]] if kt == 0 else []), signal=True)
                PT.free[pi_] = pv
            if kts[-1] == qt:
                pvs_of[(qt, m_)] = pv
            if kts[-1] == qt and m_ == 1:
                post(qt)
            return pv

        def post(qt):
            obs = OB[qt % 2]
            qsl = slice(qt * 128, (qt + 1) * 128)
            pvs = [pvs_of.pop((qt, 0)), pvs_of.pop((qt, 1))]
            O1 = C.ps[obs[0]]
            O2 = C.ps[obs[1]]
            si = sm.get()
            sc = sm.tiles[si]
            oi_ = osb.get()
            o = osb.tiles[oi_]
            ni = onb.get()
            on = onb.tiles[ni]
            e1 = P.op("dve", lambda e: e.reciprocal(sc[:, 0:1], O1[:, 256:257]), waits=pvs + [sm.free[si]])
            e2 = P.op("dve", lambda e: e.reciprocal(sc[:, 1:2], O2[:, 256:257]), waits=pvs)
            e3 = P.op("dve", lambda e: e.tensor_tensor(sc[:, 1:2], sc[:, 1:2], lsc[:, 0:1], ALU.mult), waits=[e2] + const_ev)
            e4 = P.op("dve", lambda e: e.tensor_scalar(o[:], O1[:, 0:256], sc[:, 0:1], None, ALU.mult), waits=[e1, osb.free[oi_]])
            e5 = P.op("dve", lambda e: e.scalar_tensor_tensor(o[:], O2[:, 0:256], sc[:, 1:2], o[:], ALU.mult, ALU.add), waits=[e3, e4])
            C.ps_free[obs[0]] = e5
            C.ps_free[obs[1]] = e5
            e6 = P.op("act", lambda e: e.activation(junk[:], o[:], AF.Square, accum_out=sc[:, 2:3]), waits=e5)
            e7 = P.op("dve", lambda e: e.tensor_scalar(sc[:, 2:3], sc[:, 2:3], 1.0 / 256, SUBLN_EPS, ALU.mult, ALU.add), waits=e6)
            e8 = P.op("act", lambda e: e.activation(sc[:, 2:3], sc[:, 2:3], AF.Sqrt), waits=e7)
            e9 = P.op("dve", lambda e: e.reciprocal(sc[:, 2:3], sc[:, 2:3]), waits=e8)
            e10 = P.op("dve", lambda e: e.scalar_tensor_tensor(on[:], o[:], sc[:, 2:3], gs[:], ALU.mult, ALU.mult),
                       waits=[e9, onb.free[ni]] + const_ev)
            osb.free[oi_] = e10
            sm.free[si] = e10
            psb = C.ps[0][:].bitcast(BF16)
            tev = None
            for j_ in range(2):
                tev = P.op("pe", lambda e, j_=j_: e.transpose(
                    psb[:, j_ * 128:(j_ + 1) * 128], on[:, j_ * 128:(j_ + 1) * 128], C.ident_b[:]),
                    waits=[e10, C.ps_free[0]] if j_ == 0 else None, signal=(j_ == 1))
            onb.free[ni] = tev
            cev = evac(P, "dve", OT[:, :, qsl], psb[:, 0:256].rearrange("p (a b) -> p a b", a=2), waits=[tev, oT.free[oi]])
            C.ps_free[0] = cev
            tr_evs.append(cev)

        n = len(steps)
        stage1(0)
        stage1(1)
        pv = None
        for i in range(n):
            if i + 2 < n:
                stage1(i + 2)
            pv = stage2(i)
            yield
        qT.free[qi] = pv
        kT.free[ki] = pv
        Vh.free[vi] = pv
        st = P.dma("sp", C.mT[h * 2:h * 2 + 2].rearrange("j p t -> p j t"), OT[:], oT.sems[oi], waits=tr_evs)
        oT.free[oi] = st
        yield

    def step(gen):
        if gen is None:
            return None
        try:
            next(gen)
            return gen
        except StopIteration:
            return None

    g0 = gen_proj(0)
    while g0 is not None:
        g0 = step(g0)
    for h in range(8):
        gB = gen_attn(h)
        gA = gen_proj(h + 1) if h + 1 < 8 else None
        cnt = 0
        while gA is not None or gB is not None:
            if cnt % 8 == 0:
                gA = step(gA)
            gB = step(gB)
            cnt += 1
    P.barrier([oT.free[0], oT.free[1]])


def phase_final(C):
    P = C.P
    yT = Ring(P, 2, [128, DC, TT], F32, "f_yT")
    ot = Ring(P, 2, [128, D], F32, "f_ot", dma=True)
    stores = []
    g = 0
    gen = norm_gen(C, 7, lambda i, c: yT.tiles[(i // 2) % 2][:, c, (i % 2) * 256:(i % 2 + 1) * 256],
                   dst_wait_fn=lambda i: yT.free[(i // 2) % 2])
    for i, ev in gen:
        if i % 2 == 0:
            continue
        tt = i // 2
        y = tt % 2
        last_t = None
        for ti in range(4):
            o = ot.get()
            cevs = []
            for q in range(4):
                bank = g % 2
                g += 1
                ps = C.ps[bank]
                tev = None
                for k in range(4):
                    c = q * 4 + k
                    tev = P.op("pe", lambda e, ps=ps, k=k, c=c, y=y, ti=ti: e.transpose(
                        ps[:, k * 128:(k + 1) * 128], yT.tiles[y][:, c, ti * 128:(ti + 1) * 128], C.ident_f[:]),
                        waits=[ev, C.ps_free[bank]] if k == 0 else None, signal=(k == 3))
                if q % 2:
                    cev = P.op("act", lambda e, ps=ps, o=o, q=q: e.copy(ot.tiles[o][:, q * 512:(q + 1) * 512], ps[:]), waits=[tev, ot.free[o]])
                else:
                    cev = P.op("dve", lambda e, ps=ps, o=o, q=q: e.tensor_copy(ot.tiles[o][:, q * 512:(q + 1) * 512], ps[:]), waits=[tev, ot.free[o]])
                C.ps_free[bank] = cev
                cevs.append(cev)
                last_t = tev
            r0 = tt * TT + ti * 128
            st = P.dma("sp", C.out[r0:r0 + 128, :], ot.tiles[o][:], ot.sems[o], waits=cevs)
            ot.free[o] = st
            stores.append(st)
        yT.free[y] = last_t
    P.barrier(stores)


def host_consts():
    k = np.arange(128)[:, None]
    q = np.arange(128)[None, :]
    BIG = 30000.0
    own = np.where(q >= k, -(q - k).astype(np.float32), -BIG)
    prev = np.where(k >= q, -(q + 128 - k).astype(np.float32), -BIG)
    dm1 = np.concatenate([own, prev], axis=1).astype(np.float32)
    tri = (q >= k).astype(np.float32)
    btab = np.zeros((128, 128), np.float32)
    for h in range(8):
        slope = 2.0 ** (-(h + 1))
        for dlt in range(16):
            btab[:, h * 16 + dlt] = slope * (np.arange(128) - 64 - 128 * dlt)
    return {"ident": np.eye(128, dtype=np.float32), "dm1": dm1, "tri": tri, "btab": btab}


_IN_NAMES = ["norm_gains", "ffn_w_in", "ffn_w_out", "a_w_qkv", "a_w_o", "kv_norm_gain", "b_w_kv", "b_w_q",
             "b_lambda", "b_subln_gain", "b_w_o", "final_norm_gain"]


def kernel(**inputs):
    x = np.ascontiguousarray(inputs["x"], dtype=np.float32)
    nc = build_program()
    ident = np.eye(128, dtype=np.float32)
    in_maps = []
    for c in range(N_CORES):
        m = {"x": x[c]}
        m.update(host_consts())
        for k in _IN_NAMES:
            m[k] = np.ascontiguousarray(inputs[k], dtype=np.float32)
        in_maps.append(m)
    res = run_bass_kernel_spmd(nc, in_maps, core_ids=list(range(N_CORES)))
    return np.stack([r["out"] for r in res.results], axis=0)
```

```python
import numpy as np
import concourse.bass as bass
import concourse.mybir as mybir
from concourse.bass_utils import run_bass_kernel_spmd

F32 = mybir.dt.float32
BF16 = mybir.dt.bfloat16
AF = mybir.ActivationFunctionType
ALU = mybir.AluOpType

D = 2048
S = 2048
DC = 16
F = 5504
FC = 43
NT = 4
TT = 512
NPART = 4
RMS_EPS = 1e-6
SUBLN_EPS = 1e-5
N_CORES = 8

ENGS = ("pe", "act", "dve", "pool", "sp")


class Prog:
    def __init__(self, nc):
        self.nc = nc
        self.q = {e: [] for e in ENGS}
        self.sem = {}
        self.cnt = {}
        self.waited = {e: {} for e in ENGS}
        self.nsem = 0
        self.sem_by_name = {}
        for e in ("pe", "act", "dve", "pool"):
            self.sem[e] = self.new_sem("prog_" + e)
        self.sb_off = (nc.sbuf_base + 63) // 64 * 64
        self.sb_top = nc.sbuf_top
        self.nalloc = 0

    def new_sem(self, name):
        if name in self.sem_by_name:
            return self.sem_by_name[name]
        h = self.nc.alloc_semaphore(name=name)
        self.sem_by_name[name] = h
        self.cnt[id(h)] = 0
        self.nsem += 1
        return h

    def alloc(self, shape, dtype, name=None):
        nbytes = int(np.prod(shape[1:])) * (4 if dtype == F32 else 2)
        nbytes = (nbytes + 63) // 64 * 64
        off = self.sb_off
        assert off + nbytes <= self.sb_top, ("SBUF overflow", name, off, nbytes, self.sb_top)
        self.sb_off += nbytes
        self.nalloc += 1
        t = self.nc.alloc_sbuf_tensor_at(name or f"t{self.nalloc}", list(shape), dtype, offset=off)
        return t

    def mark(self):
        return self.sb_off

    def reset(self, m):
        self.sb_off = m

    def wait(self, eng, ev):
        if ev is None:
            return
        if isinstance(ev, list):
            for x in ev:
                self.wait(eng, x)
            return
        sem, val = ev
        k = id(sem)
        if self.waited[eng].get(k, 0) >= val:
            return
        self.waited[eng][k] = val
        self.q[eng].append(lambda e: e.wait_ge(sem, val))

    def op(self, eng, fn, waits=None, signal=True):
        self.wait(eng, waits)
        if signal:
            sem = self.sem[eng]
            self.cnt[id(sem)] += 1
            val = self.cnt[id(sem)]
            self.q[eng].append(lambda e: fn(e).then_inc(sem, 1))
            return (sem, val)
        self.q[eng].append(fn)
        return None

    def dma(self, eng, out, in_, sem, waits=None, slow=False):
        self.wait(eng, waits)
        self.cnt[id(sem)] += 16
        val = self.cnt[id(sem)]
        if slow:
            self.q[eng].append(lambda e: e.dma_start(out=out, in_=in_, allow_slow_non_contiguous=True).then_inc(sem, 16))
        else:
            self.q[eng].append(lambda e: e.dma_start(out=out, in_=in_).then_inc(sem, 16))
        return (sem, val)

    def last(self, eng):
        sem = self.sem[eng]
        return (sem, self.cnt[id(sem)])

    def barrier(self, extra=None):
        evs = [self.last(e) for e in ("pe", "act", "dve", "pool")]
        if extra:
            evs = evs + [x for x in extra if x is not None]
        best = {}
        for sem, val in evs:
            if id(sem) not in best or best[id(sem)][1] < val:
                best[id(sem)] = (sem, val)
        evs = list(best.values())
        for e in ENGS:
            self.wait(e, evs)

    def finalize(self):
        nc = self.nc
        q = self.q
        with nc.Block() as block:
            @block.tensor
            def _(e):
                for f in q["pe"]:
                    f(e)

            @block.scalar
            def _(e):
                for f in q["act"]:
                    f(e)

            @block.vector
            def _(e):
                for f in q["dve"]:
                    f(e)

            @block.gpsimd
            def _(e):
                for f in q["pool"]:
                    f(e)

            @block.sync
            def _(e):
                for f in q["sp"]:
                    f(e)


class Ring:
    def __init__(self, P, n, shape, dtype, name, dma=False):
        self.n = n
        self.tiles = [P.alloc(shape, dtype, f"{name}{i}") for i in range(n)]
        self.free = [None] * n
        self.sems = [P.new_sem(f"{name}_s{i}") for i in range(n)] if dma else None
        self.i = 0

    def get(self):
        idx = self.i % self.n
        self.i += 1
        return idx


def mm_group(P, out, pairs, waits=None):
    n = len(pairs)
    ev = None
    for i, (l, r) in enumerate(pairs):
        st, sp_ = (i == 0), (i == n - 1)
        fn = (lambda e, l=l, r=r, st=st, sp_=sp_: e.matmul(out, l, r, start=st, stop=sp_))
        ev = P.op("pe", fn, waits=waits if i == 0 else None, signal=sp_)
    return ev


class Ctx:
    pass


def build_program(upto=99, single=False):
    nc = bass.Bass("TRN2", target_bir_lowering=False)
    C = Ctx()
    C.nc = nc
    dt = nc.dram_tensor
    C.x = dt("x", [S, D], F32, kind="ExternalInput").ap()
    C.norm_gains = dt("norm_gains", [2, 3, D], F32, kind="ExternalInput").ap()
    C.ffn_w_in = dt("ffn_w_in", [2, 2, D, 2 * F], F32, kind="ExternalInput").ap()
    C.ffn_w_out = dt("ffn_w_out", [2, 2, F, D], F32, kind="ExternalInput").ap()
    C.a_w_qkv = dt("a_w_qkv", [1, D, 9 * D], F32, kind="ExternalInput").ap()
    C.a_w_o = dt("a_w_o", [1, D, D], F32, kind="ExternalInput").ap()
    C.kv_norm_gain = dt("kv_norm_gain", [D], F32, kind="ExternalInput").ap()
    C.b_w_kv = dt("b_w_kv", [D, 2 * D], F32, kind="ExternalInput").ap()
    C.b_w_q = dt("b_w_q", [1, D, D], F32, kind="ExternalInput").ap()
    C.b_lambda = dt("b_lambda", [1, 4, 128], F32, kind="ExternalInput").ap()
    C.b_subln_gain = dt("b_subln_gain", [1, 256], F32, kind="ExternalInput").ap()
    C.b_w_o = dt("b_w_o", [1, D, D], F32, kind="ExternalInput").ap()
    C.final_norm_gain = dt("final_norm_gain", [D], F32, kind="ExternalInput").ap()
    C.ident_in = dt("ident", [128, 128], F32, kind="ExternalInput").ap()
    C.dm1_in = dt("dm1", [128, 256], F32, kind="ExternalInput").ap()
    C.tri_in = dt("tri", [128, 128], F32, kind="ExternalInput").ap()
    C.btab_in = dt("btab", [128, 128], F32, kind="ExternalInput").ap()
    C.out = dt("out", [S, D], F32, kind="ExternalOutput").ap()
    C.xT = dt("xT_scr", [DC, 128, S], F32).ap()
    C.mT = dt("mT_scr", [DC, 128, S], BF16).ap()
    C.kT = dt("kT_scr", [DC, 128, S], BF16).ap()
    C.V = dt("V_scr", [16, 128, D], BF16).ap()

    P = Prog(nc)
    C.P = P
    C.ps = [nc.alloc_psum_tensor(f"ps{i}", [128, 512], F32) for i in range(8)]
    C.ps_free = [None] * 8

    C.ident_f = P.alloc([128, 128], F32, "ident_f")
    C.ident_b = P.alloc([128, 128], BF16, "ident_b")
    C.ones_b = P.alloc([128, 128], BF16, "ones_b")
    C.gains = P.alloc([128, 8, DC], F32, "gains")
    C.dm1 = P.alloc([128, 2, 128], F32, "dm1")
    C.bigA = P.alloc([128, DC, S], BF16, "bigA")
    C.setup_sem = P.new_sem("setup")

    ev = P.dma("sp", C.ident_f[:], C.ident_in[:, :], C.setup_sem)
    ev = P.dma("sp", C.dm1[:], C.dm1_in.rearrange("p (a b) -> p a b", a=2), C.setup_sem)
    for w in range(6):
        l, j = divmod(w, 3)
        ev = P.dma("sp", C.gains[:, w, :], C.norm_gains[l, j].rearrange("(c p) -> p c", p=128), C.setup_sem, slow=True)
    ev = P.dma("sp", C.gains[:, 6, :], C.kv_norm_gain.rearrange("(c p) -> p c", p=128), C.setup_sem, slow=True)
    ev = P.dma("sp", C.gains[:, 7, :], C.final_norm_gain.rearrange("(c p) -> p c", p=128), C.setup_sem, slow=True)
    C.setup_ev = ev
    P.op("dve", lambda e: e.tensor_copy(C.ident_b[:], C.ident_f[:]), waits=ev)
    P.op("dve", lambda e: e.memset(C.ones_b[:], 1.0))
    P.barrier([ev])

    phases = [
        ("init", lambda: phase_init(C)),
        ("ffn00", lambda: phase_ffn(C, 0, 0)),
        ("attnA", lambda: phase_attn_a(C)),
        ("woA", lambda: phase_wo(C, C.a_w_o[0], "A")),
        ("ffn01", lambda: phase_ffn(C, 0, 1)),
        ("kv", lambda: phase_kv(C)),
        ("ffn10", lambda: phase_ffn(C, 1, 0)),
        ("attnB", lambda: phase_attn_b(C)),
        ("woB", lambda: phase_wo(C, C.b_w_o[0], "B")),
        ("ffn11", lambda: phase_ffn(C, 1, 1)),
    ]
    for i, (name, fn) in enumerate(phases):
        if i > upto:
            break
        m = P.mark()
        fn()
        P.reset(m)
    m = P.mark()
    phase_final(C)
    P.reset(m)
    P.finalize()
    return nc


def phase_init(C):
    P = C.P
    xin = Ring(P, 3, [128, D], F32, "ix", dma=True)
    xs = Ring(P, 2, [128, DC, 128], F32, "ixs", dma=True)
    stores = []
    lds = {}

    def ensure_ld(k):
        if k < S // 128 and k not in lds:
            a_ = xin.get()
            lds[k] = (a_, P.dma("sp", xin.tiles[a_][:], C.x[k * 128:(k + 1) * 128, :], xin.sems[a_], waits=xin.free[a_]))

    ensure_ld(0)
    for i in range(S // 128):
        ensure_ld(i + 1)
        a, ld = lds.pop(i)
        b = xs.get()
        evs = []
        for g in range(4):
            bank = g % 2
            ps = C.ps[bank]
            tev = None
            for k in range(4):
                c = g * 4 + k
                tev = P.op("pe", lambda e, ps=ps, k=k, c=c, a=a: e.transpose(
                    ps[:, k * 128:(k + 1) * 128], xin.tiles[a][:, c * 128:(c + 1) * 128], C.ident_f[:]),
                    waits=[ld, C.ps_free[bank]] if k == 0 else None, signal=(k == 3))
            if g % 2:
                cev = P.op("act", lambda e, ps=ps, b=b, g=g: e.copy(
                    xs.tiles[b][:, g * 4:(g + 1) * 4, :], ps[:].rearrange("p (k t) -> p k t", k=4)), waits=[tev, xs.free[b]])
            else:
                cev = P.op("dve", lambda e, ps=ps, b=b, g=g: e.tensor_copy(
                    xs.tiles[b][:, g * 4:(g + 1) * 4, :], ps[:].rearrange("p (k t) -> p k t", k=4)), waits=[tev, xs.free[b]])
            C.ps_free[bank] = cev
            evs.append(cev)
        xin.free[a] = tev
        st = P.dma("sp", C.xT[:, :, i * 128:(i + 1) * 128].rearrange("c p t -> p c t"), xs.tiles[b][:], xs.sems[b], waits=evs)
        xs.free[b] = st
        stores.append(st)
    P.barrier(stores[-2:])


def norm_gen(C, widx, dst_fn, dst_wait_fn=None):
    P = C.P
    NSUB, W = 8, 256
    xin = Ring(P, 2, [128, DC, W], F32, "nx", dma=True)
    sq = Ring(P, 3, [128, W], BF16, "nsq")
    rs = Ring(P, 2, [128, W], F32, "nrs")
    banks = [6, 7]
    lds = {}

    def load(i):
        a_ = xin.get()
        lds[i] = (a_, P.dma("sp", xin.tiles[a_][:], C.xT[:, :, i * W:(i + 1) * W].rearrange("c p t -> p c t"), xin.sems[a_],
                            waits=xin.free[a_]))

    load(0)
    for i in range(NSUB):
        if i + 1 < NSUB:
            load(i + 1)
        a, ld = lds.pop(i)
        X = xin.tiles[a]
        bank = banks[i % 2]
        ps = C.ps[bank][:, 0:W]
        mm = None
        for c in range(DC):
            k = sq.get()
            sev = P.op("act", lambda e, k=k, c=c, X=X: e.activation(sq.tiles[k][:], X[:, c, :], AF.Square),
                       waits=[ld, sq.free[k]])
            mm = P.op("pe", lambda e, k=k, c=c, ps=ps: e.matmul(ps, C.ones_b[:], sq.tiles[k][:], start=(c == 0), stop=(c == DC - 1)),
                      waits=[sev, C.ps_free[bank]] if c == 0 else [sev], signal=True)
            sq.free[k] = mm
        r = rs.get()
        R_ = rs.tiles[r]
        r1 = P.op("dve", lambda e, R_=R_, ps=ps: e.tensor_scalar(R_[:], ps, 1.0 / D, RMS_EPS, ALU.mult, ALU.add), waits=[mm, rs.free[r]])
        C.ps_free[bank] = r1
        r2 = P.op("act", lambda e, R_=R_: e.activation(R_[:], R_[:], AF.Sqrt), waits=r1)
        r3 = P.op("dve", lambda e, R_=R_: e.reciprocal(R_[:], R_[:]), waits=r2)
        dw = dst_wait_fn(i) if dst_wait_fn else None
        ev = r3
        for c in range(DC):
            ev = P.op("dve", lambda e, c=c, i=i, X=X, R_=R_: e.scalar_tensor_tensor(
                dst_fn(i, c), X[:, c, :], C.gains[:, widx, c:c + 1], R_[:], ALU.mult, ALU.mult), waits=[r3, dw] if c == 0 else None)
        xin.free[a] = ev
        rs.free[r] = ev
        yield i, ev


def norm_to_bigA(C, widx):
    evs = [ev for _, ev in norm_gen(C, widx, lambda i, c: C.bigA[:, c, i * 256:(i + 1) * 256])]
    return [evs[1], evs[3], evs[5], evs[7]]


def norm_tile(C, widx, tt, dst_fn, xin_all, sq, rstd, ld_sem, ps_bank, free_ev):
    P = C.P
    ld = P.dma("sp", xin_all[:], C.xT[:, :, tt * TT:(tt + 1) * TT].rearrange("c p t -> p c t"), ld_sem, waits=free_ev)
    ps = C.ps[ps_bank]
    mm = None
    for c in range(DC):
        k = sq.get()
        sev = P.op("act", lambda e, k=k, c=c: e.activation(sq.tiles[k][:], xin_all[:, c, :], AF.Square),
                   waits=[ld, sq.free[k]])
        mm = P.op("pe", lambda e, k=k, c=c: e.matmul(ps[:], C.ones_b[:], sq.tiles[k][:], start=(c == 0), stop=(c == DC - 1)),
                  waits=[sev, C.ps_free[ps_bank]] if c == 0 else [sev], signal=True)
        sq.free[k] = mm
    r1 = P.op("dve", lambda e: e.tensor_scalar(rstd[:], ps[:], 1.0 / D, RMS_EPS, ALU.mult, ALU.add), waits=[mm, free_ev])
    C.ps_free[ps_bank] = r1
    r2 = P.op("act", lambda e: e.activation(rstd[:], rstd[:], AF.Sqrt), waits=r1)
    r2 = P.op("dve", lambda e: e.reciprocal(rstd[:], rstd[:]), waits=r2)
    ev = r2
    for c in range(DC):
        ev = P.op("dve", lambda e, c=c: e.scalar_tensor_tensor(
            dst_fn(c), xin_all[:, c, :], C.gains[:, widx, c:c + 1], rstd[:], ALU.mult, ALU.mult), waits=r2 if c == 0 else None)
    return ev


class WStream:
    def __init__(self, P, ns=3, elems=2048, name="wst", init_wait=None):
        self.P = P
        self.st = Ring(P, ns, [128, elems], F32, name, dma=True)
        self.st.free = [init_wait] * ns

    def load(self, dst, src, a, b, dst_free):
        P = self.P
        k = self.st.get()
        view = self.st.tiles[k][:, :a * b].rearrange("p (a b) -> p a b", a=a)
        ld = P.dma("sp", view, src, self.st.sems[k], waits=self.st.free[k])
        cv = P.op("pool", lambda e: e.tensor_copy(dst, view), waits=[ld, dst_free])
        self.st.free[k] = cv
        return cv


def phase_ffn(C, l, j):
    P = C.P
    widx = l * 3 + (0 if j == 0 else 2)
    w_in = C.ffn_w_in[l, j]
    w_out = C.ffn_w_out[l, j]
    NP = 11
    aT = P.alloc([128, NP, S], BF16, "aT")
    m0 = P.mark()
    P.reset(m0 - NP * S * 2)
    hn_ev = norm_to_bigA(C, widx)
    assert P.mark() <= m0
    P.reset(m0)
    RG, RD, PF = 4, 4, 3
    ws = WStream(P, 3, 2048, "wst")
    wgu = Ring(P, RG, [128, 2, DC, 128], BF16, "wgu")
    wo = Ring(P, RD, [128, NP, 128], BF16, "wo")
    sg = Ring(P, 2, [128, TT], F32, "sg")
    xi = Ring(P, 6, [128, TT], F32, "xi", dma=True)
    xo = Ring(P, 3, [128, TT], F32, "xo", dma=True)
    psz = [11, 11, 11, 10]
    blocks = []
    c0 = 0
    for part in range(NPART):
        n = psz[part]
        for ci in range(n):
            blocks.append(("gu", part, ci, c0 + ci, n, c0))
        for jd in range(DC):
            blocks.append(("dn", part, jd, None, n, c0))
        c0 += n
    kidx = {"gu": 0, "dn": 0}
    done = {"gu": 0, "dn": 0}
    ready = {}

    def issue(m):
        kind, part, a, c, n, cb = blocks[m]
        q = kidx[kind]
        kidx[kind] += 1
        if kind == "gu":
            sl = q % RG
            assert q < RG or done["gu"] > q - RG
            e1 = ws.load(wgu.tiles[sl][:, 0, :, :], w_in[:, c * 128:(c + 1) * 128].rearrange("(k p) n -> p k n", p=128), DC, 128, wgu.free[sl])
            e2 = ws.load(wgu.tiles[sl][:, 1, :, :], w_in[:, F + c * 128:F + (c + 1) * 128].rearrange("(k p) n -> p k n", p=128), DC, 128, None)
            ready[m] = (sl, [e1, e2])
        else:
            sl = q % RD
            assert q < RD or done["dn"] > q - RD
            e1 = ws.load(wo.tiles[sl][:, :n, :], w_out[cb * 128:(cb + n) * 128, a * 128:(a + 1) * 128].rearrange("(c p) n -> p c n", p=128), n, 128, wo.free[sl])
            ready[m] = (sl, [e1])

    store_ev = {}
    PFX = 4
    lxq = {}
    lx_next = {}

    def ensure_lx(part, upto):
        k = lx_next.get(part, 0)
        while k < DC * NT and k <= upto:
            jd_, tt_ = divmod(k, NT)
            xa_ = xi.get()
            lxq[(part, k)] = (xa_, P.dma("sp", xi.tiles[xa_][:], C.xT[jd_, :, tt_ * TT:(tt_ + 1) * TT], xi.sems[xa_],
                                         waits=[xi.free[xa_], store_ev.get((jd_, tt_))]))
            k += 1
        lx_next[part] = k

    aT_free = [None] * NP * NT
    aT_ready = {}
    GB, UB, YB = [0, 1], [2, 3], [4, 5]
    it = 0
    yit = 0
    nxt = 0
    for m, (kind, part, a, c, n, cb) in enumerate(blocks):
        while nxt < len(blocks) and nxt <= m + PF:
            issue(nxt)
            nxt += 1
        sl, wev = ready.pop(m)
        if kind == "gu":
            ci = a
            w = wgu.tiles[sl]
            for tt in range(NT):
                gb = GB[it % 2]
                ub = UB[it % 2]
                it += 1
                tsl = slice(tt * TT, (tt + 1) * TT)
                gev = mm_group(P, C.ps[gb][:], [(w[:, 0, k, :], C.bigA[:, k, tsl]) for k in range(DC)],
                               waits=[wev, hn_ev[tt], C.ps_free[gb]])
                uev = mm_group(P, C.ps[ub][:], [(w[:, 1, k, :], C.bigA[:, k, tsl]) for k in range(DC)],
                               waits=[C.ps_free[ub]])
                s = sg.get()
                aev = P.op("act", lambda e, s=s, gb=gb: e.activation(sg.tiles[s][:], C.ps[gb][:], AF.Silu), waits=[gev, sg.free[s]])
                C.ps_free[gb] = aev
                dev = P.op("dve", lambda e, s=s, ub=ub, ci=ci, tsl=tsl: e.tensor_tensor(aT[:, ci, tsl], sg.tiles[s][:], C.ps[ub][:], ALU.mult),
                           waits=[aev, uev, aT_free[ci * NT + tt], hn_ev[NT - 1]])
                C.ps_free[ub] = dev
                sg.free[s] = dev
                aT_ready[(ci, tt)] = dev
            wgu.free[sl] = uev
            done["gu"] += 1
        else:
            jd = a
            w = wo.tiles[sl]
            for tt in range(NT):
                yb = YB[yit % 2]
                yit += 1
                tsl = slice(tt * TT, (tt + 1) * TT)
                ensure_lx(part, jd * NT + tt + PFX)
                xa, lx = lxq.pop((part, jd * NT + tt))
                yev = mm_group(P, C.ps[yb][:], [(w[:, ci, :], aT[:, ci, tsl]) for ci in range(n)],
                               waits=[wev, C.ps_free[yb]] + [aT_ready[(ci, tt)] for ci in range(n)])
                b = xo.get()
                dev = P.op("dve", lambda e, xa=xa, b=b, yb=yb: e.scalar_tensor_tensor(
                    xo.tiles[b][:], C.ps[yb][:], 0.5, xi.tiles[xa][:], ALU.mult, ALU.add), waits=[yev, lx, xo.free[b]])
                C.ps_free[yb] = dev
                xi.free[xa] = dev
                st = P.dma("sp", C.xT[jd, :, tsl], xo.tiles[b][:], xo.sems[b], waits=dev)
                xo.free[b] = st
                store_ev[(jd, tt)] = st
                if jd == DC - 1:
                    for ci in range(n):
                        aT_free[ci * NT + tt] = yev
            wo.free[sl] = yev
            done["dn"] += 1
    P.barrier(list(store_ev.values()))


def evac(P, eng, out, in_, waits=None):
    if eng == "act":
        return P.op("act", lambda e: e.copy(out, in_), waits=waits)
    return P.op(eng, lambda e: e.tensor_copy(out, in_), waits=waits)


def run_norm(C, widx, tagname):
    m = C.P.mark()
    hn_ev = norm_to_bigA(C, widx)
    return hn_ev, m


def phase_wo(C, w_o, tagname):
    P = C.P
    lds_tt = []
    for q in range(4):
        ldsem = P.new_sem("wold%d" % q)
        lds_tt.append(P.dma("sp", C.bigA[:, :, q * TT:(q + 1) * TT], C.mT[:, :, q * TT:(q + 1) * TT].rearrange("c p t -> p c t"), ldsem))
    R, PF = 4, 3
    ws = WStream(P, 3, 2048, "wst")
    wr = Ring(P, R, [128, DC, 128], BF16, "wow")
    xi = Ring(P, 6, [128, TT], F32, "xi", dma=True)
    xo = Ring(P, 3, [128, TT], F32, "xo", dma=True)
    ready = {}
    done = [0]

    def issue(m):
        sl = m % R
        assert m < R or done[0] > m - R
        ready[m] = (sl, ws.load(wr.tiles[sl][:], w_o[:, m * 128:(m + 1) * 128].rearrange("(k p) n -> p k n", p=128), DC, 128, wr.free[sl]))

    YB = [4, 5]
    stores = []
    nxt = 0
    yit = 0
    lxq = {}
    lxn = [0]

    def ensure_lx(upto):
        k = lxn[0]
        while k < DC * NT and k <= upto:
            jd_, tt_ = divmod(k, NT)
            xa_ = xi.get()
            lxq[k] = (xa_, P.dma("sp", xi.tiles[xa_][:], C.xT[jd_, :, tt_ * TT:(tt_ + 1) * TT], xi.sems[xa_], waits=xi.free[xa_]))
            k += 1
        lxn[0] = k

    for jd in range(DC):
        while nxt < DC and nxt <= jd + PF:
            issue(nxt)
            nxt += 1
        sl, wev = ready.pop(jd)
        w = wr.tiles[sl]
        for tt in range(NT):
            yb = YB[yit % 2]
            yit += 1
            tsl = slice(tt * TT, (tt + 1) * TT)
            ensure_lx(jd * NT + tt + 4)
            xa, lx = lxq.pop(jd * NT + tt)
            yev = mm_group(P, C.ps[yb][:], [(w[:, k, :], C.bigA[:, k, tsl]) for k in range(DC)],
                           waits=[wev, C.ps_free[yb], lds_tt[tt]])
            b = xo.get()
            dev = P.op("dve", lambda e, xa=xa, b=b, yb=yb: e.tensor_tensor(
                xo.tiles[b][:], C.ps[yb][:], xi.tiles[xa][:], ALU.add), waits=[yev, lx, xo.free[b]])
            C.ps_free[yb] = dev
            xi.free[xa] = dev
            st = P.dma("sp", C.xT[jd, :, tsl], xo.tiles[b][:], xo.sems[b], waits=dev)
            xo.free[b] = st
            stores.append(st)
        wr.free[sl] = yev
        done[0] += 1
    P.barrier(stores)


def phase_attn_a(C):
    P = C.P
    w_qkv = C.a_w_qkv[0]
    hn_ev, m_norm = run_norm(C, 1, "A")
    P.reset(m_norm)
    hn_all = list(hn_ev)
    R = 3
    ws = WStream(P, 3, 2048, "wst", init_wait=hn_ev[-1])
    wr = Ring(P, R, [128, 3, DC, 128], BF16, "wqkv")
    qkv = Ring(P, 2, [128, 3, S], BF16, "qkvT")
    Vb = Ring(P, 2, [128, 16, 128], BF16, "Vb")
    ET = Ring(P, 2, [128, 2, 128], F32, "ET")
    ex = Ring(P, 4, [128, 2, 128], F32, "ex")
    PT = Ring(P, 4, [128, 2, 128], BF16, "PT")
    accO = P.alloc([128, S], F32, "accO")
    accD = P.alloc([128, S], F32, "accD")
    rD = P.alloc([128, S], F32, "rD")
    mh = Ring(P, 2, [128, S], BF16, "mh", dma=True)
    dil = [1, 4, 16]
    scale = 128.0 ** -0.5
    items = [(h, g) for h in range(16) for g in range(3)]
    NI = len(items)
    ready = {}
    done = [0]
    issued = [0]
    st8 = {}
    sh = {"pit": 0, "sit": 0, "oit": 0, "acc_ev": None, "mh_ev": None, "acc_last": None}

    def issue_upto(k):
        while issued[0] < NI and issued[0] <= k:
            m = issued[0]
            h, g = items[m]
            sl = m % R
            assert m < R or done[0] > m - R
            evs = []
            for s_ in range(3):
                col = ((s_ * 3 + g) * 16 + h) * 128
                evs.append(ws.load(wr.tiles[sl][:, s_, :, :], w_qkv[:, col:col + 128].rearrange("(k p) n -> p k n", p=128), DC, 128,
                                   wr.free[sl] if s_ == 0 else None))
            ready[m] = (sl, evs)
            issued[0] += 1

    def gen_proj(m):
        h, g = items[m]
        issue_upto(m + 1)
        sl, wev = ready.pop(m)
        w = wr.tiles[sl]
        d = dil[g]
        qi = qkv.get()
        T3 = qkv.tiles[qi]
        pev = []
        gev = None
        for s_ in range(3):
            for tt in range(NT):
                pb = sh["pit"] % 2
                sh["pit"] += 1
                tsl = slice(tt * TT, (tt + 1) * TT)
                gev = mm_group(P, C.ps[pb][:], [(w[:, s_, k, :], C.bigA[:, k, tsl]) for k in range(DC)],
                               waits=[wev, C.ps_free[pb]] + hn_all)
                n_ = TT // d
                dst = T3[:, s_, :].rearrange("p (c j) -> p c j", c=d)[:, :, tt * n_:(tt + 1) * n_]
                src = C.ps[pb][:].rearrange("p (j c) -> p c j", c=d)
                eng = "act"
                cev = evac(P, eng, dst, src, waits=[gev, qkv.free[qi]])
                C.ps_free[pb] = cev
                pev.append(cev)
                if s_ * NT + tt == 11:
                    wr.free[sl] = gev
                    done[0] += 1
                    st8[m] = (qi, T3, pev)
                yield

    def gen_attn(m):
        h, g = items[m]
        qi, T3, pev = st8.pop(m)
        d = dil[g]
        bpc = 16 // d
        slope = 2.0 ** (-8.0 * (h + 1) / 16.0)
        vi = Vb.get()
        vev = []
        for q4 in range(4):
            psb = C.ps[2][:].bitcast(BF16)
            tev = None
            for k in range(4):
                b = q4 * 4 + k
                tev = P.op("pe", lambda e, psb=psb, k=k, b=b, T3=T3: e.transpose(
                    psb[:, k * 128:(k + 1) * 128], T3[:, 2, b * 128:(b + 1) * 128], C.ident_b[:]),
                    waits=(pev + [C.ps_free[2]]) if k == 0 else None, signal=(k == 3))
            cev = evac(P, "dve", Vb.tiles[vi][:, q4 * 4:(q4 + 1) * 4, :], psb[:, 0:512].rearrange("p (k t) -> p k t", k=4),
                       waits=[tev, Vb.free[vi]])
            C.ps_free[2] = cev
            vev.append(cev)
            if q4 % 2 == 1:
                yield
        ei = ET.get()
        etev = P.op("act", lambda e, ei=ei, sc=slope * d: e.activation(ET.tiles[ei][:], C.dm1[:], AF.Exp, scale=sc), waits=ET.free[ei])
        stage = {}

        def stage1(qb):
            has_prev = (qb % bpc) != 0
            nk = 2 if has_prev else 1
            sb = 3 + (sh["sit"] % 2)
            sh["sit"] += 1
            stp = C.ps[sb]
            qs = T3[:, 0, qb * 128:(qb + 1) * 128]
            sev = P.op("pe", lambda e, stp=stp, qs=qs, qb=qb, T3=T3: e.matmul(
                stp[:, 0:128], T3[:, 1, qb * 128:(qb + 1) * 128], qs, start=True, stop=True),
                waits=pev + [C.ps_free[sb]], signal=not has_prev)
            if has_prev:
                sev = P.op("pe", lambda e, stp=stp, qs=qs, qb=qb, T3=T3: e.matmul(
                    stp[:, 128:256], T3[:, 1, (qb - 1) * 128:qb * 128], qs, start=True, stop=True))
            xi_ = ex.get()
            xev = P.op("act", lambda e, xi_=xi_, stp=stp, nk=nk: e.activation(
                ex.tiles[xi_][:, 0:nk, :], stp[:, 0:nk * 128].rearrange("p (a b) -> p a b", a=nk), AF.Exp, scale=scale),
                waits=[sev, ex.free[xi_]])
            C.ps_free[sb] = xev
            pi_ = PT.get()
            mev = P.op("dve", lambda e, xi_=xi_, pi_=pi_, ei=ei, nk=nk: e.tensor_tensor(
                PT.tiles[pi_][:, 0:nk, :], ex.tiles[xi_][:, 0:nk, :], ET.tiles[ei][:, 0:nk, :], ALU.mult),
                waits=[xev, etev, PT.free[pi_]])
            ex.free[xi_] = mev
            stage[qb] = (has_prev, nk, pi_, mev)

        def stage2(qb):
            has_prev, nk, pi_, mev = stage.pop(qb)
            oit = sh["oit"]
            ob = 5 + ((oit // 2) % 2)
            half = oit % 2
            sh["oit"] += 1
            odp = C.ps[ob]
            o_sl = odp[:, half * 128:(half + 1) * 128]
            d_sl = odp[:, 256 + half * 128:256 + (half + 1) * 128]
            pv = P.op("pe", lambda e, o_sl=o_sl, vi=vi, qb=qb, pi_=pi_, nk=nk: e.matmul(
                o_sl, Vb.tiles[vi][:, qb, :], PT.tiles[pi_][:, 0, :], start=True, stop=(nk == 1)),
                waits=[mev] + vev + ([C.ps_free[ob]] if half == 0 else []), signal=False)
            if has_prev:
                pv = P.op("pe", lambda e, o_sl=o_sl, vi=vi, qb=qb, pi_=pi_: e.matmul(
                    o_sl, Vb.tiles[vi][:, qb - 1, :], PT.tiles[pi_][:, 1, :], start=False, stop=True), signal=False)
            pv = P.op("pe", lambda e, d_sl=d_sl, pi_=pi_, nk=nk: e.matmul(
                d_sl, C.ones_b[:], PT.tiles[pi_][:, 0, :], start=True, stop=(nk == 1)), signal=(nk == 1))
            if has_prev:
                pv = P.op("pe", lambda e, d_sl=d_sl, pi_=pi_: e.matmul(
                    d_sl, C.ones_b[:], PT.tiles[pi_][:, 1, :], start=False, stop=True))
            PT.free[pi_] = pv
            if half == 1:
                q0 = qb - 1
                if d == 1:
                    dO = accO[:, q0 * 128:(q0 + 2) * 128]
                    dD = accD[:, q0 * 128:(q0 + 2) * 128]
                    sO = odp[:, 0:256]
                    sD = odp[:, 256:512]
                elif d == 4:
                    c_ = q0 // 4
                    j0 = (q0 % 4) * 128
                    dO = accO[:].rearrange("p (j c) -> p c j", c=4)[:, c_, j0:j0 + 256]
                    dD = accD[:].rearrange("p (j c) -> p c j", c=4)[:, c_, j0:j0 + 256]
                    sO = odp[:, 0:256]
                    sD = odp[:, 256:512]
                else:
                    dO = accO[:].rearrange("p (j c) -> p c j", c=16)[:, q0:q0 + 2, :]
                    dD = accD[:].rearrange("p (j c) -> p c j", c=16)[:, q0:q0 + 2, :]
                    sO = odp[:, 0:256].rearrange("p (a b) -> p a b", a=2)
                    sD = odp[:, 256:512].rearrange("p (a b) -> p a b", a=2)
                if g == 0:
                    a1 = evac(P, "act", dO, sO, waits=[pv, sh["mh_ev"]])
                    a2 = evac(P, "act", dD, sD, waits=[pv, sh["mh_ev"]])
                else:
                    a1 = P.op("dve", lambda e, dO=dO, sO=sO: e.tensor_tensor(dO, dO, sO, ALU.add), waits=[pv, sh["acc_ev"]])
                    a2 = P.op("dve", lambda e, dD=dD, sD=sD: e.tensor_tensor(dD, dD, sD, ALU.add), waits=[pv, sh["acc_ev"]])
                C.ps_free[ob] = [a1, a2]
                sh["acc_last"] = [a1, a2]
            return pv, mev

        stage1(0)
        stage1(1)
        pv = mev = None
        for qb in range(16):
            if qb + 2 < 16:
                stage1(qb + 2)
            pv, mev = stage2(qb)
            yield
        qkv.free[qi] = [pv]
        Vb.free[vi] = pv
        ET.free[ei] = mev
        sh["acc_ev"] = sh["acc_last"]
        if g == 2:
            mi = mh.get()
            al = sh["acc_last"]
            r1 = P.op("dve", lambda e: e.reciprocal(rD[:], accD[:]), waits=al)
            r2 = P.op("dve", lambda e, mi=mi: e.tensor_tensor(mh.tiles[mi][:], accO[:], rD[:], ALU.mult), waits=[r1, mh.free[mi]] + al)
            sh["mh_ev"] = r2
            st = P.dma("sp", C.mT[h], mh.tiles[mi][:], mh.sems[mi], waits=r2)
            mh.free[mi] = st
        yield

    def step(gen):
        if gen is None:
            return None
        try:
            next(gen)
            return gen
        except StopIteration:
            return None

    g0 = gen_proj(0)
    while g0 is not None:
        g0 = step(g0)
    for m in range(NI):
        gB = gen_attn(m)
        gA = gen_proj(m + 1) if m + 1 < NI else None
        while gA is not None or gB is not None:
            gA = step(gA)
            gB = step(gB)
            gB = step(gB)
    P.barrier([mh.free[0], mh.free[1]])


def phase_kv(C):
    P = C.P
    w_kv = C.b_w_kv
    hn_ev, m_norm = run_norm(C, 6, "KV")
    P.reset(m_norm)
    hn_all = list(hn_ev)
    R, PF = 4, 3
    ws = WStream(P, 3, 2048, "wst", init_wait=hn_ev[-1])
    wr = Ring(P, R, [128, DC, 128], BF16, "wk")
    ko = Ring(P, 2, [128, S], BF16, "ko", dma=True)
    wv = Ring(P, 2, [128, DC, 512], BF16, "wv")
    vo = Ring(P, 3, [128, 512], BF16, "vo", dma=True)
    ready = {}
    done = [0]

    def issue(m):
        sl = m % R
        assert m < R or done[0] > m - R
        ready[m] = (sl, ws.load(wr.tiles[sl][:], w_kv[:, m * 128:(m + 1) * 128].rearrange("(k p) n -> p k n", p=128), DC, 128, wr.free[sl]))

    def issue_v(j):
        sl = j % 2
        evs = []
        for q in range(4):
            evs.append(ws.load(wv.tiles[sl][:, q * 4:(q + 1) * 4, :],
                               w_kv[q * 512:(q + 1) * 512, D + j * 512:D + (j + 1) * 512].rearrange("(k p) n -> p k n", p=128),
                               4, 512, wv.free[sl] if q == 0 else None))
        return sl, evs

    nxt = 0
    pit = 0
    stores = []
    for m in range(16):
        while nxt < 16 and nxt <= m + PF:
            issue(nxt)
            nxt += 1
        sl, wev = ready.pop(m)
        w = wr.tiles[sl]
        ki = ko.get()
        cevs = []
        for tt in range(NT):
            pb = pit % 2
            pit += 1
            tsl = slice(tt * TT, (tt + 1) * TT)
            gev = mm_group(P, C.ps[pb][:], [(w[:, k, :], C.bigA[:, k, tsl]) for k in range(DC)],
                           waits=[wev, C.ps_free[pb]] + hn_all)
            cev = evac(P, "act" if tt % 2 == 0 else "dve", ko.tiles[ki][:, tsl], C.ps[pb][:], waits=[gev, ko.free[ki]])
            C.ps_free[pb] = cev
            cevs.append(cev)
        wr.free[sl] = gev
        done[0] += 1
        st = P.dma("sp", C.kT[m], ko.tiles[ki][:], ko.sems[ki], waits=cevs)
        ko.free[ki] = st
        stores.append(st)
    vready = {0: issue_v(0)}
    for j in range(4):
        if j + 1 < 4:
            vready[j + 1] = issue_v(j + 1)
        sl, wev = vready.pop(j)
        w = wv.tiles[sl]
        for ti in range(16):
            pb = pit % 2
            pit += 1
            gev = mm_group(P, C.ps[pb][:], [(C.bigA[:, k, ti * 128:(ti + 1) * 128], w[:, k, :]) for k in range(DC)],
                           waits=wev + [C.ps_free[pb]] + hn_all)
            vi = vo.get()
            cev = evac(P, "act" if ti % 2 == 0 else "dve", vo.tiles[vi][:], C.ps[pb][:], waits=[gev, vo.free[vi]])
            C.ps_free[pb] = cev
            st = P.dma("sp", C.V[ti, :, j * 512:(j + 1) * 512], vo.tiles[vi][:], vo.sems[vi], waits=cev)
            vo.free[vi] = st
            stores.append(st)
        wv.free[sl] = gev
    P.barrier(stores)


def phase_attn_b(C):
    import math
    P = C.P
    w_q = C.b_w_q[0]
    lam_init = 0.8 - 0.6 * math.exp(-0.3 * 1)
    scale = 128.0 ** -0.5
    hn_ev, m_norm = run_norm(C, 4, "B")
    P.reset(m_norm)
    hn_all = list(hn_ev)
    R, PF = 3, 2
    ws = WStream(P, 3, 2048, "wst", init_wait=hn_ev[-1])
    wr = Ring(P, R, [128, 2, DC, 128], BF16, "wq")
    qT = Ring(P, 2, [128, 2, S], BF16, "qT")
    kT = Ring(P, 2, [128, 2, S], BF16, "kTb", dma=True)
    Vh = Ring(P, 2, [128, 16, 257], BF16, "Vh", dma=True)
    PT = Ring(P, 14, [128, 128], BF16, "PTb")
    osb = Ring(P, 2, [128, 256], F32, "osb")
    onb = Ring(P, 2, [128, 256], BF16, "onb")
    sm = Ring(P, 2, [128, 4], F32, "smb")
    junk = P.alloc([128, 256], F32, "junk")
    oT = Ring(P, 2, [128, 2, S], BF16, "oTb", dma=True)
    lam = P.alloc([128, 4, 128], F32, "lam")
    ltmp = P.alloc([128, 2, 128], F32, "ltmp")
    lsc = P.alloc([128, 4], F32, "lsc")
    gs = P.alloc([128, 256], F32, "gsub")
    tri_b = P.alloc([128, 128], BF16, "tri_b")
    tri_f = P.alloc([128, 128], F32, "tri_f")
    btab = P.alloc([128, 128], F32, "btab")
    csem = P.new_sem("bconst")
    cw = hn_ev[-1]
    P.dma("sp", lam[:], C.b_lambda[0].rearrange("a b -> (a b)").partition_broadcast(128).rearrange("p (a b) -> p a b", a=4), csem, waits=cw)
    P.dma("sp", gs[:], C.b_subln_gain[0].partition_broadcast(128), csem)
    P.dma("sp", tri_f[:], C.tri_in[:, :], csem)
    cl = P.dma("sp", btab[:], C.btab_in[:, :], csem)
    P.op("dve", lambda e: e.tensor_copy(tri_b[:], tri_f[:]), waits=cl)
    P.op("dve", lambda e: e.tensor_tensor(ltmp[:, 0, :], lam[:, 0, :], lam[:, 1, :], ALU.mult))
    l1 = P.op("dve", lambda e: e.tensor_tensor(ltmp[:, 1, :], lam[:, 2, :], lam[:, 3, :], ALU.mult))
    l2 = P.op("dve", lambda e: e.reduce_sum(lsc[:, 0:2], ltmp[:], mybir.AxisListType.X), waits=l1)
    l3 = P.op("act", lambda e: e.activation(lsc[:, 2:4], lsc[:, 0:2], AF.Exp), waits=l2)
    l4 = P.op("dve", lambda e: e.tensor_tensor(lsc[:, 0:1], lsc[:, 3:4], lsc[:, 2:3], ALU.subtract), waits=l3)
    l5 = P.op("dve", lambda e: e.tensor_scalar_add(lsc[:, 0:1], lsc[:, 0:1], -lam_init), waits=l4)
    g1 = P.op("dve", lambda e: e.tensor_scalar_mul(gs[:], gs[:], 1.0 - lam_init), waits=cl)
    const_ev = [l5, g1]
    for i in range(2):
        g1 = P.op("dve", lambda e, i=i: e.memset(Vh.tiles[i][:, :, 256:257], 1.0), waits=cw)
    ones_ev = g1
    ready = {}
    done = [0]

    issued = [0]

    def issue_upto(k):
        while issued[0] < 8 and issued[0] <= k:
            h = issued[0]
            sl = h % R
            assert h < R or done[0] > h - R
            evs = []
            for m_ in range(2):
                col = (h * 2 + m_) * 128
                evs.append(ws.load(wr.tiles[sl][:, m_, :, :], w_q[:, col:col + 128].rearrange("(k p) n -> p k n", p=128), DC, 128,
                                   wr.free[sl] if m_ == 0 else None))
            ready[h] = (sl, evs)
            issued[0] += 1

    OB = [[4, 5], [6, 7]]
    sh = {"pit": 0, "scnt": 0}
    hst = {}

    def gen_proj(h):
        issue_upto(h + 1)
        sl, wev = ready.pop(h)
        w = wr.tiles[sl]
        ki = kT.get()
        kld = P.dma("sp", kT.tiles[ki][:], C.kT[h * 2:h * 2 + 2].rearrange("m p t -> p m t"), kT.sems[ki], waits=[kT.free[ki], cw])
        vi = Vh.get()
        vld = P.dma("sp", Vh.tiles[vi][:, :, 0:256], C.V[:, :, h * 256:(h + 1) * 256].rearrange("t p e -> p t e"), Vh.sems[vi],
                    waits=[Vh.free[vi], cw])
        qi = qT.get()
        Q = qT.tiles[qi]
        pev = []
        for m_ in range(2):
            for tt in range(NT):
                pb = 0
                tsl = slice(tt * TT, (tt + 1) * TT)
                gev = mm_group(P, C.ps[pb][:], [(w[:, m_, k, :], C.bigA[:, k, tsl]) for k in range(DC)],
                               waits=wev + [C.ps_free[pb]] + hn_all)
                cev = evac(P, "dve", Q[:, m_, tsl], C.ps[pb][:], waits=[gev, qT.free[qi]])
                C.ps_free[pb] = cev
                pev.append(cev)
                if m_ == 1 and tt == NT - 1:
                    wr.free[sl] = gev
                    done[0] += 1
                    hst[h] = (ki, kld, vi, vld, qi, Q, pev)
                yield

    def gen_attn(h):
        ki, kld, vi, vld, qi, Q, pev = hst.pop(h)
        oi = oT.get()
        OT = oT.tiles[oi]
        tr_evs = []
        steps = []
        for qt in range(16):
            for m_ in range(2):
                for k0 in range(0, qt + 1, 4):
                    steps.append((qt, m_, list(range(k0, min(k0 + 4, qt + 1)))))
        pend = {}
        pvs_of = {}

        def stage1(i):
            qt, m_, kts = steps[i]
            qsl = slice(qt * 128, (qt + 1) * 128)
            sbank = 1 + (sh["scnt"] % 3)
            sh["scnt"] += 1
            sev = None
            for i_, kt in enumerate(kts):
                stp = C.ps[sbank][:, i_ * 128:(i_ + 1) * 128]
                sev = P.op("pe", lambda e, stp=stp, m_=m_, kt=kt, qsl=qsl: e.matmul(
                    stp, kT.tiles[ki][:, m_, kt * 128:(kt + 1) * 128], Q[:, m_, qsl], start=True, stop=True),
                    waits=(pev + [kld, C.ps_free[sbank]]) if i_ == 0 else None, signal=(i_ == len(kts) - 1))
            xevs = []
            pis = []
            last_x = None
            for i_, kt in enumerate(kts):
                stp = C.ps[sbank][:, i_ * 128:(i_ + 1) * 128]
                pi_ = PT.get()
                pis.append(pi_)
                bcol = h * 16 + (qt - kt)
                xev = P.op("act", lambda e, pi_=pi_, stp=stp, bcol=bcol: e.activation(
                    PT.tiles[pi_][:], stp, AF.Exp, bias=btab[:, bcol:bcol + 1], scale=scale),
                    waits=[sev, PT.free[pi_], cl])
                last_x = xev
                if kt == qt:
                    xev = P.op("dve", lambda e, pi_=pi_: e.tensor_tensor(PT.tiles[pi_][:], PT.tiles[pi_][:], tri_b[:], ALU.mult), waits=xev)
                xevs.append(xev)
            C.ps_free[sbank] = last_x
            pend[i] = (pis, xevs)

        def stage2(i):
            qt, m_, kts = steps[i]
            obs = OB[qt % 2]
            ob = obs[m_]
            pis, xevs = pend.pop(i)
            pv = None
            for i_, kt in enumerate(kts):
                pi_ = pis[i_]
                pv = P.op("pe", lambda e, ob=ob, pi_=pi_, kt=kt, qt=qt: e.matmul(
                    C.ps[ob][:, 0:257], PT.tiles[pi_][:], Vh.tiles[vi][:, kt, :], start=(kt == 0), stop=(kt == qt)),
                    waits=[xevs[i_], vld, ones_ev] + ([C.ps_free[ob]] if kt == 0 else []), signal=True)
                PT.free[pi_] = pv
            if kts[-1] == qt:
                pvs_of[(qt, m_)] = pv
            if kts[-1] == qt and m_ == 1:
                post(qt)
            return pv

        def post(qt):
            obs = OB[qt % 2]
            qsl = slice(qt * 128, (qt + 1) * 128)
            pvs = [pvs_of.pop((qt, 0)), pvs_of.pop((qt, 1))]
            O1 = C.ps[obs[0]]
            O2 = C.ps[obs[1]]
            si = sm.get()
            sc = sm.tiles[si]
            oi_ = osb.get()
            o = osb.tiles[oi_]
            ni = onb.get()
            on = onb.tiles[ni]
            e1 = P.op("dve", lambda e: e.reciprocal(sc[:, 0:1], O1[:, 256:257]), waits=pvs + [sm.free[si]])
            e2 = P.op("dve", lambda e: e.reciprocal(sc[:, 1:2], O2[:, 256:257]), waits=pvs)
            e3 = P.op("dve", lambda e: e.tensor_tensor(sc[:, 1:2], sc[:, 1:2], lsc[:, 0:1], ALU.mult), waits=[e2] + const_ev)
            e4 = P.op("dve", lambda e: e.tensor_scalar(o[:], O1[:, 0:256], sc[:, 0:1], None, ALU.mult), waits=[e1, osb.free[oi_]])
            e5 = P.op("dve", lambda e: e.scalar_tensor_tensor(o[:], O2[:, 0:256], sc[:, 1:2], o[:], ALU.mult, ALU.add), waits=[e3, e4])
            C.ps_free[obs[0]] = e5
            C.ps_free[obs[1]] = e5
            e6 = P.op("dve", lambda e: e.tensor_tensor(junk[:], o[:], o[:], ALU.mult), waits=[e5, sh.get("junk_ev")])
            e6 = P.op("dve", lambda e: e.reduce_sum(sc[:, 2:3], junk[:], mybir.AxisListType.X), waits=e6)
            sh["junk_ev"] = e6
            e7 = P.op("dve", lambda e: e.tensor_scalar(sc[:, 2:3], sc[:, 2:3], 1.0 / 256, SUBLN_EPS, ALU.mult, ALU.add), waits=e6)
            e8 = P.op("act", lambda e: e.activation(sc[:, 2:3], sc[:, 2:3], AF.Ln), waits=e7)
            e9 = P.op("act", lambda e: e.activation(sc[:, 2:3], sc[:, 2:3], AF.Exp, scale=-0.5), waits=e8)
            e10 = P.op("dve", lambda e: e.scalar_tensor_tensor(on[:], o[:], sc[:, 2:3], gs[:], ALU.mult, ALU.mult),
                       waits=[e9, onb.free[ni]] + const_ev)
            osb.free[oi_] = e10
            sm.free[si] = e10
            psb = C.ps[0][:].bitcast(BF16)
            tev = None
            for j_ in range(2):
                tev = P.op("pe", lambda e, j_=j_: e.transpose(
                    psb[:, j_ * 128:(j_ + 1) * 128], on[:, j_ * 128:(j_ + 1) * 128], C.ident_b[:]),
                    waits=[e10, C.ps_free[0]] if j_ == 0 else None, signal=(j_ == 1))
            onb.free[ni] = tev
            cev = evac(P, "dve", OT[:, :, qsl], psb[:, 0:256].rearrange("p (a b) -> p a b", a=2), waits=[tev, oT.free[oi]])
            C.ps_free[0] = cev
            tr_evs.append(cev)

        n = len(steps)
        stage1(0)
        stage1(1)
        pv = None
        for i in range(n):
            if i + 2 < n:
                stage1(i + 2)
            pv = stage2(i)
            yield
        qT.free[qi] = pv
        kT.free[ki] = pv
        Vh.free[vi] = pv
        st = P.dma("sp", C.mT[h * 2:h * 2 + 2].rearrange("j p t -> p j t"), OT[:], oT.sems[oi], waits=tr_evs)
        oT.free[oi] = st
        yield

    def step(gen):
        if gen is None:
            return None
        try:
            next(gen)
            return gen
        except StopIteration:
            return None

    g0 = gen_proj(0)
    while g0 is not None:
        g0 = step(g0)
    for h in range(8):
        gB = gen_attn(h)
        gA = gen_proj(h + 1) if h + 1 < 8 else None
        cnt = 0
        while gA is not None or gB is not None:
            if cnt % 8 == 0:
                gA = step(gA)
            gB = step(gB)
            cnt += 1
    P.barrier([oT.free[0], oT.free[1]])


def phase_final(C):
    P = C.P
    yT = Ring(P, 2, [128, DC, TT], F32, "f_yT")
    ot = Ring(P, 2, [128, D], F32, "f_ot", dma=True)
    stores = []
    g = 0
    gen = norm_gen(C, 7, lambda i, c: yT.tiles[(i // 2) % 2][:, c, (i % 2) * 256:(i % 2 + 1) * 256],
                   dst_wait_fn=lambda i: yT.free[(i // 2) % 2])
    for i, ev in gen:
        if i % 2 == 0:
            continue
        tt = i // 2
        y = tt % 2
        last_t = None
        for ti in range(4):
            o = ot.get()
            cevs = []
            for q in range(4):
                bank = g % 2
                g += 1
                ps = C.ps[bank]
                tev = None
                for k in range(4):
                    c = q * 4 + k
                    tev = P.op("pe", lambda e, ps=ps, k=k, c=c, y=y, ti=ti: e.transpose(
                        ps[:, k * 128:(k + 1) * 128], yT.tiles[y][:, c, ti * 128:(ti + 1) * 128], C.ident_f[:]),
                        waits=[ev, C.ps_free[bank]] if k == 0 else None, signal=(k == 3))
                if q % 2:
                    cev = P.op("act", lambda e, ps=ps, o=o, q=q: e.copy(ot.tiles[o][:, q * 512:(q + 1) * 512], ps[:]), waits=[tev, ot.free[o]])
                else:
                    cev = P.op("dve", lambda e, ps=ps, o=o, q=q: e.tensor_copy(ot.tiles[o][:, q * 512:(q + 1) * 512], ps[:]), waits=[tev, ot.free[o]])
                C.ps_free[bank] = cev
                cevs.append(cev)
                last_t = tev
            r0 = tt * TT + ti * 128
            st = P.dma("sp", C.out[r0:r0 + 128, :], ot.tiles[o][:], ot.sems[o], waits=cevs)
            ot.free[o] = st
            stores.append(st)
        yT.free[y] = last_t
    P.barrier(stores)


def host_consts():
    k = np.arange(128)[:, None]
    q = np.arange(128)[None, :]
    BIG = 30000.0
    own = np.where(q >= k, -(q - k).astype(np.float32), -BIG)
    prev = np.where(k >= q, -(q + 128 - k).astype(np.float32), -BIG)
    dm1 = np.concatenate([own, prev], axis=1).astype(np.float32)
    tri = (q >= k).astype(np.float32)
    btab = np.zeros((128, 128), np.float32)
    for h in range(8):
        slope = 2.0 ** (-(h + 1))
        for dlt in range(16):
            btab[:, h * 16 + dlt] = slope * (np.arange(128) - 64 - 128 * dlt)
    return {"ident": np.eye(128, dtype=np.float32), "dm1": dm1, "tri": tri, "btab": btab}


_IN_NAMES = ["norm_gains", "ffn_w_in", "ffn_w_out", "a_w_qkv", "a_w_o", "kv_norm_gain", "b_w_kv", "b_w_q",
             "b_lambda", "b_subln_gain", "b_w_o", "final_norm_gain"]


def kernel(**inputs):
    x = np.ascontiguousarray(inputs["x"], dtype=np.float32)
    nc = build_program()
    ident = np.eye(128, dtype=np.float32)
    in_maps = []
    for c in range(N_CORES):
        m = {"x": x[c]}
        m.update(host_consts())
        for k in _IN_NAMES:
            m[k] = np.ascontiguousarray(inputs[k], dtype=np.float32)
        in_maps.append(m)
    res = run_bass_kernel_spmd(nc, in_maps, core_ids=list(range(N_CORES)))
    return np.stack([r["out"] for r in res.results], axis=0)
```
